# Optimizing a Trainium2 kernel written in Bass

```python
import math
import jax, jax.numpy as jnp
from jax import lax
import numpy as np


D_MODEL = 1024
BATCH = 4
SEQ = 8192
DEPTH = 4

MIX_WIDTH = 2 * D_MODEL
CONV_WIDTH = MIX_WIDTH // 2
CONV_K = 3
ATTN_HEAD_DIM = 64
ATTN_HEADS = (MIX_WIDTH - CONV_WIDTH) // ATTN_HEAD_DIM
ATTN_WIDTH = ATTN_HEADS * ATTN_HEAD_DIM
DILATED_PATTERNS = ((128, 1), (512, 4), (2048, 16))
ATTN_BLOCK = 128
REL_BUCKETS = 32
REL_MAX_DISTANCE = 2048
HGRN_HEADS = 16
HGRN_KEY_DIM = 128
HGRN_VAL_DIM = MIX_WIDTH // HGRN_HEADS
HGRN_CHUNK = 64
EPS = 1e-6
N_EVEN = (DEPTH + 1) // 2
N_ODD = DEPTH // 2
EVEN_SPLITS = (CONV_WIDTH, CONV_WIDTH, CONV_WIDTH, ATTN_WIDTH, ATTN_WIDTH, ATTN_WIDTH, MIX_WIDTH)
ODD_SPLITS = (HGRN_HEADS * HGRN_KEY_DIM, HGRN_HEADS * HGRN_KEY_DIM, HGRN_HEADS * HGRN_VAL_DIM, MIX_WIDTH)
EVEN_IN = sum(EVEN_SPLITS)
ODD_IN = sum(ODD_SPLITS)

kernel_name = 'hybrid_conv_dilattn_hgrn2_trunk'


def split_cols(t, sizes):
    idx = np.cumsum(sizes)[:-1].tolist()
    return jnp.split(t, idx, axis=-1)


def rms_norm(x, gain):
    xf = x.astype(jnp.float32)
    return xf * lax.rsqrt(jnp.mean(xf * xf, axis=-1, keepdims=True) + EPS) * gain.astype(jnp.float32)


def t5_bucket(distance):
    max_exact = REL_BUCKETS // 2
    scaled = jnp.log(jnp.maximum(distance, max_exact).astype(jnp.float32) / max_exact) / math.log(REL_MAX_DISTANCE / max_exact)
    large = jnp.minimum(max_exact + (scaled * (REL_BUCKETS - max_exact)).astype(jnp.int32), REL_BUCKETS - 1)
    return jnp.where(distance < max_exact, distance, large)


def dilated_window_attention(q, k, v, rel_bias, window, dilation):
    b, h, s, dh = q.shape
    blk = ATTN_BLOCK
    n_back = window // dilation
    assert n_back <= blk
    l_pad = -(-s // (dilation * blk)) * blk
    s_pad = l_pad * dilation
    nb = l_pad // blk

    def to_blocks(t):
        t = jnp.pad(t, ((0, 0), (0, 0), (0, s_pad - s), (0, 0)))
        t = t.reshape(b, h, l_pad, dilation, dh).transpose(0, 1, 3, 2, 4)
        return t.reshape(b, h, dilation, nb, blk, dh)

    def with_prev(t):
        prev = jnp.pad(t[:, :, :, :-1], ((0, 0), (0, 0), (0, 0), (1, 0), (0, 0), (0, 0)))
        return jnp.concatenate([prev, t], axis=4)

    qb = to_blocks(q)
    kk = with_prev(to_blocks(k))
    vv = with_prev(to_blocks(v))
    qi = jnp.arange(blk)[:, None]
    kj = jnp.arange(2 * blk)[None, :]
    delta = blk + qi - kj
    band = (delta >= 0) & (delta <= n_back)
    bucket = t5_bucket(jnp.maximum(delta, 0) * dilation)
    bias = rel_bias.astype(jnp.float32)[bucket].transpose(2, 0, 1)
    not_before_start = (jnp.arange(nb)[:, None, None] > 0) | (kj[None] >= blk)
    mask = band[None] & not_before_start
    logits = jnp.einsum('bhrnqd,bhrnkd->bhrnqk', qb, kk) + bias[None, :, None, None]
    logits = jnp.where(mask, logits, -jnp.inf)
    m = jnp.max(logits, axis=-1, keepdims=True)
    p = jnp.exp(logits - m)
    den = jnp.sum(p, axis=-1, keepdims=True)
    o = jnp.einsum('bhrnqk,bhrnkd->bhrnqd', p, vv) / den
    lse = (m + jnp.log(den))[..., 0]

    def from_blocks(t, tail):
        t = t.reshape((b, h, dilation, l_pad) + tail)
        t = jnp.moveaxis(t, 2, 3).reshape((b, h, s_pad) + tail)
        return t[:, :, :s]

    return from_blocks(o, (dh,)), from_blocks(lse, ())


def conv_attention_mixer(h, w_in, conv_w, q_gain, k_gain, rel_bias, w_out):
    b, s, _ = h.shape
    gate_b, gate_c, xa, q, k, v, z = split_cols(h @ w_in.astype(h.dtype), EVEN_SPLITS)
    u = gate_c * xa
    conv = lax.conv_general_dilated(u, conv_w[:, None, :].astype(u.dtype), window_strides=(1,),
                                    padding=((CONV_K - 1, 0),), dimension_numbers=('NWC', 'WIO', 'NWC'),
                                    feature_group_count=CONV_WIDTH)
    y_conv = gate_b * conv
    def heads(t, gain):
        return rms_norm(t.reshape(b, s, ATTN_HEADS, ATTN_HEAD_DIM), gain).transpose(0, 2, 1, 3)
    qh = heads(q, q_gain) * (ATTN_HEAD_DIM ** -0.5)
    kh = heads(k, k_gain)
    vh = v.astype(jnp.float32).reshape(b, s, ATTN_HEADS, ATTN_HEAD_DIM).transpose(0, 2, 1, 3)
    outs, lses = [], []
    for window, dilation in DILATED_PATTERNS:
        o_p, lse_p = dilated_window_attention(qh, kh, vh, rel_bias, window, dilation)
        outs.append(o_p)
        lses.append(lse_p)
    weights = jax.nn.softmax(jnp.stack(lses), axis=0)
    o = jnp.sum(weights[..., None] * jnp.stack(outs), axis=0)
    y_attn = o.transpose(0, 2, 1, 3).reshape(b, s, ATTN_WIDTH)
    y = jnp.concatenate([y_conv, y_attn.astype(y_conv.dtype)], axis=-1) * jax.nn.silu(z)
    return y @ w_out.astype(y.dtype)


def hgrn2_chunk_scan(q, k, v, log_f):
    b, h, s, dk = q.shape
    dv = v.shape[-1]
    c = HGRN_CHUNK
    nc = s // c

    def chunks(t):
        return t.reshape(b, h, nc, c, t.shape[-1]).transpose(2, 0, 1, 3, 4)

    causal = jnp.tril(jnp.ones((c, c), dtype=bool))

    def step(state, inp):
        qc, kc, vc, gc = inp
        gcum = jnp.cumsum(gc, axis=2)
        o_inter = jnp.einsum('bhtk,bhkv->bhtv', qc * jnp.exp(gcum), state)
        diff = gcum[:, :, :, None, :] - gcum[:, :, None, :, :]
        decay = jnp.exp(jnp.where(causal[:, :, None], diff, -jnp.inf))
        scores = jnp.einsum('bhtk,bhsk,bhtsk->bhts', qc, kc, decay)
        o_intra = jnp.einsum('bhts,bhsv->bhtv', scores, vc)
        g_last = gcum[:, :, -1:, :]
        new_state = state * jnp.exp(g_last[:, :, 0, :, None]) + jnp.einsum('bhsk,bhsv->bhkv', kc * jnp.exp(g_last - gcum), vc)
        return new_state, o_inter + o_intra

    state0 = jnp.zeros((b, h, dk, dv), jnp.float32)
    _, o = lax.scan(step, state0, (chunks(q), chunks(k), chunks(v), chunks(log_f)))
    return o.transpose(1, 2, 0, 3, 4).reshape(b, h, s, dv)


def hgrn2_mixer(h, w_in, lower_bound, o_gain, w_out):
    b, s, _ = h.shape
    q, f_pre, i, z = split_cols(h @ w_in.astype(h.dtype), ODD_SPLITS)
    f_pre = f_pre.astype(jnp.float32)
    lb = lower_bound
    log_f = jnp.logaddexp(jnp.log(lb), jnp.log1p(-lb) + jax.nn.log_sigmoid(f_pre))
    k = (1.0 - lb) * jax.nn.sigmoid(-f_pre)

    def heads(t, dim):
        return t.astype(jnp.float32).reshape(b, s, HGRN_HEADS, dim).transpose(0, 2, 1, 3)

    o = hgrn2_chunk_scan(heads(jax.nn.silu(q), HGRN_KEY_DIM), heads(k, HGRN_KEY_DIM),
                         heads(i, HGRN_VAL_DIM), heads(log_f, HGRN_KEY_DIM))
    o = rms_norm(o.transpose(0, 2, 1, 3), o_gain.reshape(HGRN_HEADS, HGRN_VAL_DIM)).reshape(b, s, MIX_WIDTH)
    y = o * jax.nn.silu(z)
    return y @ w_out.astype(y.dtype)


def setup_inputs(seed: int = 0) -> dict:
    key = jax.random.key(seed)
    ks = jax.random.split(key, 13)
    f32 = jnp.float32
    nrm = jax.random.normal
    return {
        'x': nrm(ks[0], (BATCH, SEQ, D_MODEL), f32),
        'ln_even': 1.0 + 0.1 * nrm(ks[1], (N_EVEN, D_MODEL), f32),
        'w_in_even': nrm(ks[2], (N_EVEN, D_MODEL, EVEN_IN), f32) * D_MODEL ** -0.5,
        'conv_w': nrm(ks[3], (N_EVEN, CONV_K, CONV_WIDTH), f32) * CONV_K ** -0.5,
        'q_gain': 1.0 + 0.1 * nrm(ks[4], (N_EVEN, ATTN_HEAD_DIM), f32),
        'k_gain': 1.0 + 0.1 * nrm(ks[5], (N_EVEN, ATTN_HEAD_DIM), f32),
        'w_out_even': nrm(ks[6], (N_EVEN, MIX_WIDTH, D_MODEL), f32) * MIX_WIDTH ** -0.5,
        'rel_bias': 0.5 * nrm(ks[7], (REL_BUCKETS, ATTN_HEADS), f32),
        'ln_odd': 1.0 + 0.1 * nrm(ks[8], (N_ODD, D_MODEL), f32),
        'w_in_odd': nrm(ks[9], (N_ODD, D_MODEL, ODD_IN), f32) * D_MODEL ** -0.5,
        'lower_bounds': 0.5 * nrm(ks[10], (N_ODD, HGRN_HEADS * HGRN_KEY_DIM), f32),
        'o_gain': 1.0 + 0.1 * nrm(ks[11], (N_ODD, MIX_WIDTH), f32),
        'w_out_odd': nrm(ks[12], (N_ODD, MIX_WIDTH, D_MODEL), f32) * MIX_WIDTH ** -0.5,
    }


def reference(x, ln_even, w_in_even, conv_w, q_gain, k_gain, w_out_even, rel_bias,
              ln_odd, w_in_odd, lower_bounds, o_gain, w_out_odd):
    lbs = jnp.cumsum(jax.nn.softmax(lower_bounds.astype(jnp.float32), axis=0), axis=0)
    lbs = lbs - lbs[0:1]
    for layer in range(DEPTH):
        j = layer // 2
        if layer % 2 == 0:
            hn = rms_norm(x, ln_even[j])
            delta = conv_attention_mixer(hn, w_in_even[j], conv_w[j], q_gain[j], k_gain[j], rel_bias, w_out_even[j])
        else:
            hn = rms_norm(x, ln_odd[j])
            delta = hgrn2_mixer(hn, w_in_odd[j], lbs[j], o_gain[j], w_out_odd[j])
        x = x + delta.astype(x.dtype)
    return x
```

```python
import math
from contextlib import ExitStack

import numpy as np
import concourse.bass as bass
import concourse.mybir as mybir
from concourse.bass_utils import run_bass_kernel_spmd

F32 = mybir.dt.float32
BF16 = mybir.dt.bfloat16
AF = mybir.ActivationFunctionType
ALU = mybir.AluOpType

D = 1024
S = 8192
NL = 4
EPS = 1e-6
TT = 512
NT = S // TT
NEG = -30000.0

PC_LN = 0
PC_CONV = 32
PC_QG = 80
PC_KG = 82
PC_LB = 84
PC_OG = 116
NPAR = 148


import types


def freeze(fn):
    if fn.__closure__ is None:
        return fn
    cells = []
    for c in fn.__closure__:
        try:
            cells.append(types.CellType(c.cell_contents))
        except ValueError:
            cells.append(c)
    return types.FunctionType(fn.__code__, fn.__globals__, fn.__name__, fn.__defaults__, tuple(cells))


class Buf:
    def __init__(self, name, sem=None):
        self.name = name
        self.w = None
        self.r = []
        self.sem = sem


class Eng:
    def __init__(self, name):
        self.name = name
        self.ops = []
        self.sem = None
        self.waited = {}


class Prog:
    def __init__(self, nc, stack):
        self.nc = nc
        self.engs = {n: Eng(n) for n in ("pe", "act", "dve", "pool", "sp")}
        self.sems = {}
        self.semcount = {}
        self.free_sems = []
        self.nsem = 0
        for n in self.engs:
            k = "E_" + n
            self.sems[k] = stack.enter_context(nc.semaphore(k))
            self.semcount[k] = 0
            self.engs[n].sem = k
        self.stack = stack
        self.dram_pending = []

    def get_sem(self):
        if self.free_sems:
            return self.free_sems.pop()
        k = "D_%d" % self.nsem
        self.nsem += 1
        self.sems[k] = self.stack.enter_context(self.nc.semaphore(k))
        self.semcount[k] = 0
        return k

    def put_sem(self, k):
        self.free_sems.append(k)

    def _waits(self, eng, deps):
        e = self.engs[eng]
        for d in deps:
            if d is None:
                continue
            key, val = d
            if e.waited.get(key, 0) >= val:
                continue
            e.waited[key] = val
            h = self.sems[key]
            e.ops.append(lambda E, h=h, val=val: E.wait_ge(h, val))

    def _deps(self, reads, writes):
        deps = []
        for b in reads:
            deps.append(b.w)
        for b in writes:
            deps.append(b.w)
            deps.extend(b.r)
        return deps

    def _commit(self, tok, reads, writes):
        for b in reads:
            b.r.append(tok)
        for b in writes:
            b.w = tok
            b.r = []

    def op(self, eng, fn, reads=(), writes=()):
        fn = freeze(fn)
        self._waits(eng, self._deps(reads, writes))
        e = self.engs[eng]
        key = e.sem
        self.semcount[key] += 1
        tok = (key, self.semcount[key])
        h = self.sems[key]
        e.ops.append(lambda E, fn=fn, h=h: fn(E).then_inc(h, 1))
        self._commit(tok, reads, writes)
        return tok

    def group(self, fns, reads=(), writes=()):
        fns = [freeze(f) for f in fns]
        self._waits("pe", self._deps(reads, writes))
        e = self.engs["pe"]
        for fn in fns[:-1]:
            e.ops.append(lambda E, fn=fn: fn(E))
        key = e.sem
        self.semcount[key] += 1
        tok = (key, self.semcount[key])
        h = self.sems[key]
        e.ops.append(lambda E, fn=fns[-1], h=h: fn(E).then_inc(h, 1))
        self._commit(tok, reads, writes)
        return tok

    def dma(self, eng, sem, out, in_, reads=(), writes=(), dram_write=False):
        self._waits(eng, self._deps(reads, writes))
        e = self.engs[eng]
        self.semcount[sem] += 16
        tok = (sem, self.semcount[sem])
        h = self.sems[sem]
        e.ops.append(lambda E, out=out, in_=in_, h=h: E.dma_start(out=out, in_=in_).then_inc(h, 16))
        self._commit(tok, reads, writes)
        if dram_write:
            self.dram_pending.append(tok)
        return tok

    def load(self, buf, out, in_):
        return self.dma("sp", buf.sem, out, in_, writes=[buf])

    def barrier(self):
        toks = [(k, v) for k, v in self.semcount.items() if v > 0]
        for n in self.engs:
            self._waits(n, toks)
        self.dram_pending = []

    def emit(self):
        nc = self.nc
        engs = self.engs
        with nc.Block() as block:
            @block.tensor
            def _(E):
                for f in engs["pe"].ops:
                    f(E)

            @block.scalar
            def _(E):
                for f in engs["act"].ops:
                    f(E)

            @block.vector
            def _(E):
                for f in engs["dve"].ops:
                    f(E)

            @block.gpsimd
            def _(E):
                for f in engs["pool"].ops:
                    f(E)

            @block.sync
            def _(E):
                for f in engs["sp"].ops:
                    f(E)
        for e in engs.values():
            e.ops = []


def even_src_cols():
    src = []
    for j in range(8):
        src += [1024 + 128 * j, 2048 + 128 * j, 128 * j, 6144 + 128 * j]
    for j in range(8):
        src += [3072 + 128 * j, 4096 + 128 * j, 5120 + 128 * j, 7168 + 128 * j]
    return src


def odd_src_cols():
    src = []
    for h in range(16):
        src += [128 * h, 2048 + 128 * h, 6144 + 128 * h]
    for c in range(16):
        src.append(4096 + 128 * c)
    return src


class Ctx:
    pass


_UID = [0]


def U(n):
    _UID[0] += 1
    return "%s_u%d" % (n, _UID[0])


def build(nlayers=NL):
    nc = bass.Bass("TRN2", target_bir_lowering=False)
    C = Ctx()
    C.nc = nc
    xT = nc.dram_tensor("xT", [D, S], F32, kind="ExternalInput").ap()
    w_in_even = nc.dram_tensor("w_in_even", [2, D, 8192], F32, kind="ExternalInput").ap()
    w_in_odd = nc.dram_tensor("w_in_odd", [2, D, 8192], F32, kind="ExternalInput").ap()
    w_out_even = nc.dram_tensor("w_out_even", [2, 2048, D], F32, kind="ExternalInput").ap()
    w_out_odd = nc.dram_tensor("w_out_odd", [2, 2048, D], F32, kind="ExternalInput").ap()
    par_d = nc.dram_tensor("par", [128, NPAR], F32, kind="ExternalInput").ap()
    bias_d = nc.dram_tensor("biasT", [16, 128, 768], F32, kind="ExternalInput").ap()
    tri_d = nc.dram_tensor("tri", [128, 64], F32, kind="ExternalInput").ap()
    yT_out = nc.dram_tensor("outT", [D, S], F32, kind="ExternalOutput").ap()

    C.wbi = [nc.dram_tensor("wbi%d" % l, [D, 8192], BF16).ap() for l in range(nlayers)]
    C.wbo = [nc.dram_tensor("wbo%d" % l, [2048, D], BF16).ap() for l in range(nlayers)]
    xs = [nc.dram_tensor("xs%d" % i, [D, S], F32).ap() for i in range(2)]
    C.yT = nc.dram_tensor("yTs", [2048, S], BF16).ap()
    C.t1 = nc.dram_tensor("t1s", [2048, S], BF16).ap()
    C.t2 = nc.dram_tensor("t2s", [2048, S], BF16).ap()
    C.t3 = nc.dram_tensor("t3s", [2048, S], F32).ap()
    C.t4 = nc.dram_tensor("t4s", [S, 2048], BF16).ap()
    C.w_in = [w_in_even, w_in_odd]
    C.biasT = bias_d
    C.w_out = [w_out_even, w_out_odd]

    with ExitStack() as gs:
        P = Prog(nc, gs)
        C.P = P
        sbt = lambda st, n, s, d: st.enter_context(nc.sbuf_tensor(U(n), s, d))
        C.pf = [gs.enter_context(nc.psum_tensor("pf%d" % i, [128, 512], F32)) for i in range(6)]
        C.pfb = [Buf("pf%d" % i) for i in range(6)]
        C.pb = [gs.enter_context(nc.psum_tensor("pb%d" % i, [128, 1024], BF16)) for i in range(2)]
        C.pbb = [Buf("pb%d" % i) for i in range(2)]
        par = sbt(gs, "par", [128, NPAR], F32)
        C.par = par
        ones_n = sbt(gs, "ones_n", [128, 128], BF16)
        ones_h = sbt(gs, "ones_h", [128, 128], BF16)
        blk1 = sbt(gs, "blk1", [128, 128], BF16)
        onA = sbt(gs, "onA", [128, 128], BF16)
        onB = sbt(gs, "onB", [128, 128], BF16)
        ident = sbt(gs, "ident", [128, 128], BF16)
        identf = sbt(gs, "identf", [128, 128], F32)
        tri = sbt(gs, "tri", [128, 64], F32)
        ones_f = sbt(gs, "ones_f", [128, 64], F32)
        lbt = sbt(gs, "lbt", [128, 2, 16], F32)
        C.ones_n, C.ones_h, C.blk1, C.onA, C.onB, C.ident, C.tri, C.ones_f, C.lbt = ones_n, ones_h, blk1, onA, onB, ident, tri, ones_f, lbt
        cb = Buf("consts", P.get_sem())
        C.cb = cb
        P.load(cb, par[:], par_d[:, :])
        P.load(cb, tri[:], tri_d[:, :])
        P.op("pool", lambda E: E.memset(ones_n[:], 1.0 / 1024), writes=[cb])
        P.op("pool", lambda E: E.memset(ones_h[:], 1.0 / 128), writes=[cb])
        P.op("pool", lambda E: E.memset(ones_f[:], 1.0), writes=[cb])
        P.op("pool", lambda E: E.memset(blk1[:], 0.0), writes=[cb])
        P.op("pool", lambda E: E.memset(blk1[0:64, 0:64], 1.0), writes=[cb])
        P.op("pool", lambda E: E.memset(blk1[64:128, 64:128], 1.0), writes=[cb])
        P.op("pool", lambda E: E.memset(onA[:], 0.0), writes=[cb])
        P.op("pool", lambda E: E.memset(onA[:, 0:64], 1.0), writes=[cb])
        P.op("pool", lambda E: E.memset(onB[:], 0.0), writes=[cb])
        P.op("pool", lambda E: E.memset(onB[:, 64:128], 1.0), writes=[cb])
        P.op("pool", lambda E: E.memset(identf[:], 0.0), writes=[cb])
        P.op("pool", lambda E: E.affine_select(out=identf[:], in_=identf[:], pattern=[[-1, 128]],
                                               compare_op=ALU.not_equal, fill=1.0, base=0, channel_multiplier=1), writes=[cb])
        P.op("pool", lambda E: E.tensor_copy(out=ident[:], in_=identf[:]), writes=[cb])
        P.op("pool", lambda E: E.memset(lbt[:, 0, :], 0.0), writes=[cb])
        P.op("dve", lambda E: E.tensor_tensor(out=lbt[:, 1, :], in0=par[:, PC_LB:PC_LB + 16], in1=par[:, PC_LB + 16:PC_LB + 32], op=ALU.subtract), writes=[cb])
        P.op("act", lambda E: E.activation(out=lbt[:, 1, :], in_=lbt[:, 1, :], func=AF.Exp), writes=[cb])
        P.op("dve", lambda E: E.tensor_scalar(out=lbt[:, 1, :], in0=lbt[:, 1, :], scalar1=1.0, scalar2=None, op0=ALU.add), writes=[cb])
        P.op("dve", lambda E: E.reciprocal(out=lbt[:, 1, :], in_=lbt[:, 1, :]), writes=[cb])
        P.barrier()
        P.emit()

        for l in range(nlayers):
            precast(C, l)
        import os
        kstop = int(os.environ.get("KSTOP", "99"))
        for l in range(nlayers):
            xin = xT if l == 0 else xs[(l - 1) % 2]
            xout = yT_out if l == nlayers - 1 else xs[l % 2]
            last = (l == nlayers - 1)
            if last and kstop < 1:
                break
            if l % 2 == 0:
                phaseA_even(C, l, xin)
                if last and kstop < 2:
                    break
                phaseB_even(C, l)
            else:
                phaseA_odd(C, l, xin)
                if last and kstop < 2:
                    break
                phaseB_odd(C, l)
            if last and kstop < 3:
                break
            phaseC(C, l, xin, xout)
        P.barrier()
        P.emit()
    return nc


def precast(C, l):
    nc, P = C.nc, C.P
    j = l // 2
    src_cols = even_src_cols() if l % 2 == 0 else odd_src_cols()
    w_in = C.w_in[l % 2]
    w_out = C.w_out[l % 2]
    with ExitStack() as st:
        stg = [st.enter_context(nc.sbuf_tensor(U("pc_f"), [128, 4096], F32)) for i in range(2)]
        stb = [st.enter_context(nc.sbuf_tensor(U("pc_b"), [128, 4096], BF16)) for i in range(2)]
        fb = [Buf("pcf%d" % i, P.get_sem()) for i in range(2)]
        bb = [Buf("pcb%d" % i, P.get_sem()) for i in range(2)]
        engs = ["dve", "pool", "act"]
        n = 0
        def cast(n, s):
            e = engs[n % 3]
            if e == "act":
                P.op(e, lambda E: E.copy(out=stb[s][:], in_=stg[s][:]), reads=[fb[s]], writes=[bb[s]])
            else:
                P.op(e, lambda E: E.tensor_copy(out=stb[s][:], in_=stg[s][:]), reads=[fb[s]], writes=[bb[s]])

        for g in range(16):
            s = n % 2
            f3 = stg[s][:].rearrange("p (k c) -> p k c", k=8)
            for c in range(4):
                sc = src_cols[g * 4 + c]
                P.load(fb[s], f3[:, :, c * 128:(c + 1) * 128],
                       w_in[j, :, sc:sc + 128].rearrange("(k p) c -> p k c", p=128))
            cast(n, s)
            P.dma("pool", bb[s].sem, C.wbi[l][:, g * 512:(g + 1) * 512].rearrange("(k p) c -> p k c", p=128),
                  stb[s][:].rearrange("p (k c) -> p k c", k=8), reads=[bb[s]], dram_write=True)
            n += 1
        for g in range(4):
            s = n % 2
            P.load(fb[s], stg[s][:].rearrange("p (k c) -> p k c", k=16),
                   w_out[j, :, g * 256:(g + 1) * 256].rearrange("(k p) c -> p k c", p=128))
            cast(n, s)
            P.dma("pool", bb[s].sem, C.wbo[l][:, g * 256:(g + 1) * 256].rearrange("(k p) c -> p k c", p=128),
                  stb[s][:].rearrange("p (k c) -> p k c", k=16), reads=[bb[s]], dram_write=True)
            n += 1
        P.barrier()
        P.emit()
        for b in fb + bb:
            P.put_sem(b.sem)


class BankPool:
    def __init__(self, C, idxs):
        self.C = C
        self.idxs = list(idxs)
        self.n = 0

    def get(self):
        i = self.idxs[self.n % len(self.idxs)]
        self.n += 1
        return self.C.pf[i], self.C.pfb[i]


def emit_norm(C, l, xt, xtb, sq, sqb, hn, hnb, tmp, tmpb, banks):
    P = C.P
    par = C.par
    P.op("act", lambda E: E.activation(out=sq[:], in_=xt[:], func=AF.Square), reads=[xtb], writes=[sqb])
    ps, psb = banks.get()
    sq3 = sq[:].rearrange("p (k t) -> p k t", k=8)
    xt3 = xt[:].rearrange("p (k t) -> p k t", k=8)
    hn3 = hn[:].rearrange("p (k t) -> p k t", k=8)
    P.group([(lambda E, k=k: E.matmul(ps[:], C.ones_n[:], sq3[:, k, :], start=(k == 0), stop=(k == 7))) for k in range(8)],
            reads=[sqb, C.cb], writes=[psb])
    P.op("act", lambda E: E.activation(out=tmp[:], in_=ps[:], func=AF.Sqrt, bias=EPS, scale=1.0), reads=[psb], writes=[tmpb])
    P.op("dve", lambda E: E.reciprocal(out=tmp[:], in_=tmp[:]), writes=[tmpb])
    for k in range(8):
        P.op("dve", lambda E, k=k: E.scalar_tensor_tensor(out=hn3[:, k, :], in0=xt3[:, k, :], scalar=par[:, PC_LN + l * 8 + k:PC_LN + l * 8 + k + 1],
                                                          in1=tmp[:], op0=ALU.mult, op1=ALU.mult),
             reads=[xtb, tmpb, C.cb], writes=[hnb])


def phaseA_even(C, l, xin):
    nc, P = C.nc, C.P
    j = l // 2
    par = C.par
    with ExitStack() as st:
        sb = lambda n, s, d: st.enter_context(nc.sbuf_tensor(U(n), s, d))
        xt = [sb("xt%d" % i, [128, 8 * TT], F32) for i in range(2)]
        xtb = [Buf("xt%d" % i, P.get_sem()) for i in range(2)]
        sq = sb("sq", [128, 8 * TT], BF16); sqb = Buf("sq")
        hn = [sb("hn%d" % i, [128, 8 * TT], BF16) for i in range(2)]
        hnb = [Buf("hn%d" % i) for i in range(2)]
        rs = sb("rs", [128, TT], F32); rsb = Buf("rs")
        wt = [sb("wt%d" % i, [128, 8 * 512], BF16) for i in range(3)]
        wtb = [Buf("wt%d" % i, P.get_sem()) for i in range(3)]
        ubuf = [sb("ub%d" % i, [128, TT + 2], F32) for i in range(8)]
        ubb = [Buf("ub%d" % i) for i in range(8)]
        stg = {k: sb("stg_" + k, [128, 8 * TT], BF16) for k in ("y", "q", "k", "v", "z")}
        stgb = {k: Buf("stg_" + k, P.get_sem()) for k in stg}
        NTMP = 3
        tA = [sb("tA%d" % i, [128, TT], F32) for i in range(NTMP)]; tAb = [Buf("tA%d" % i) for i in range(NTMP)]
        tB = [sb("tB%d" % i, [128, TT], F32) for i in range(NTMP)]; tBb = [Buf("tB%d" % i) for i in range(NTMP)]
        tC = [sb("tC%d" % i, [128, TT], F32) for i in range(NTMP)]; tCb = [Buf("tC%d" % i) for i in range(NTMP)]
        tD = [sb("tD%d" % i, [128, TT], F32) for i in range(NTMP)]; tDb = [Buf("tD%d" % i) for i in range(NTMP)]
        tE = [sb("tE%d" % i, [128, TT], BF16) for i in range(NTMP)]; tEb = [Buf("tE%d" % i) for i in range(NTMP)]
        tF = [sb("tF%d" % i, [128, TT], BF16) for i in range(NTMP)]; tFb = [Buf("tF%d" % i) for i in range(NTMP)]
        banks = BankPool(C, range(6))
        for i in range(8):
            P.op("pool", lambda E, i=i: E.memset(ubuf[i][:, 0:2], 0.0), writes=[ubb[i]])
        P.wait_dram = None
        P._waits("sp", P.dram_pending)

        def xload(i):
            P.load(xtb[i % 2], xt[i % 2][:].rearrange("p (k t) -> p k t", k=8),
                   xin[:, i * TT:(i + 1) * TT].rearrange("(k p) t -> p k t", p=128))

        nw = [0]

        def wload(g):
            s = nw[0] % 3
            nw[0] += 1
            P.load(wtb[s], wt[s][:].rearrange("p (k c) -> p k c", k=8),
                   C.wbi[l][:, g * 512:(g + 1) * 512].rearrange("(k p) c -> p k c", p=128))
            return s

        xload(0)
        wq = [wload(0), wload(1)]
        nt = [0]
        for i in range(NT):
            if i + 1 < NT:
                xload(i + 1)
            hb = i % 2
            emit_norm(C, l, xt[i % 2], xtb[i % 2], sq, sqb, hn[hb], hnb[hb], rs, rsb, banks)
            hn3 = hn[hb][:].rearrange("p (k t) -> p k t", k=8)
            deferred = None
            for g in range(16):
                s = wq.pop(0)
                gi = i * 16 + g + 2
                if gi < NT * 16:
                    wq.append(wload(gi % 16))
                w3 = wt[s][:].rearrange("p (k c) -> p k c", k=8)
                pcs = []
                for c in range(4):
                    ps, psb = banks.get()
                    P.group([(lambda E, k=k, c=c, ps=ps: E.matmul(ps[:], w3[:, k, c * 128:(c + 1) * 128], hn3[:, k, :], start=(k == 0), stop=(k == 7))) for k in range(8)],
                            reads=[wtb[s], hnb[hb]], writes=[psb])
                    pcs.append((ps, psb))
                if deferred is not None:
                    deferred()
                    deferred = None
                ti = nt[0] % NTMP
                nt[0] += 1
                if g < 8:
                    jj = g
                    (pgc, pgcb), (pxa, pxab), (pgb, pgbb), (pz, pzb) = pcs
                    ub, ubB = ubuf[jj], ubb[jj]
                    P.op("act", lambda E, ti=ti, pxa=pxa: E.copy(out=tA[ti][:], in_=pxa[:]), reads=[pxab], writes=[tAb[ti]])
                    P.op("dve", lambda E, ti=ti, pgc=pgc, ub=ub: E.tensor_tensor(out=ub[:, 2:TT + 2], in0=pgc[:], in1=tA[ti][:], op=ALU.mult),
                         reads=[pgcb, tAb[ti]], writes=[ubB])
                    cws = [par[:, PC_CONV + (j * 3 + k) * 8 + jj:PC_CONV + (j * 3 + k) * 8 + jj + 1] for k in range(3)]
                    cw = cws.__getitem__
                    P.op("dve", lambda E, ti=ti, ub=ub, cw=cw: E.tensor_scalar(out=tB[ti][:], in0=ub[:, 2:TT + 2], scalar1=cw(2), scalar2=None, op0=ALU.mult),
                         reads=[ubB, C.cb], writes=[tBb[ti]])
                    P.op("dve", lambda E, ti=ti, ub=ub, cw=cw: E.scalar_tensor_tensor(out=tB[ti][:], in0=ub[:, 1:TT + 1], scalar=cw(1), in1=tB[ti][:], op0=ALU.mult, op1=ALU.add),
                         reads=[ubB], writes=[tBb[ti]])
                    P.op("dve", lambda E, ti=ti, ub=ub, cw=cw: E.scalar_tensor_tensor(out=tB[ti][:], in0=ub[:, 0:TT], scalar=cw(0), in1=tB[ti][:], op0=ALU.mult, op1=ALU.add),
                         reads=[ubB], writes=[tBb[ti]])
                    P.op("act", lambda E, ti=ti, pz=pz: E.activation(out=tC[ti][:], in_=pz[:], func=AF.Silu), reads=[pzb], writes=[tCb[ti]])
                    P.op("dve", lambda E, ti=ti, pgb=pgb: E.tensor_tensor(out=tD[ti][:], in0=pgb[:], in1=tB[ti][:], op=ALU.mult),
                         reads=[pgbb, tBb[ti]], writes=[tDb[ti]])
                    P.op("pool", lambda E, ti=ti, jj=jj: E.tensor_tensor(out=stg["y"][:, jj * TT:(jj + 1) * TT], in0=tD[ti][:], in1=tC[ti][:], op=ALU.mult),
                         reads=[tDb[ti], tCb[ti]], writes=[stgb["y"]])
                    P.op("pool", lambda E, ub=ub: E.tensor_copy(out=ub[:, 0:2], in_=ub[:, TT:TT + 2]), writes=[ubB])
                else:
                    jj = g - 8
                    (pq, pqb), (pk, pkb), (pv, pvb), (pz, pzb) = pcs
                    P.op("act", lambda E, ti=ti, pq=pq: E.copy(out=tA[ti][:], in_=pq[:]), reads=[pqb], writes=[tAb[ti]])
                    P.op("act", lambda E, ti=ti, pk=pk: E.copy(out=tB[ti][:], in_=pk[:]), reads=[pkb], writes=[tBb[ti]])
                    P.op("act", lambda E, jj=jj, pv=pv: E.copy(out=stg["v"][:, jj * TT:(jj + 1) * TT], in_=pv[:]), reads=[pvb], writes=[stgb["v"]])
                    P.op("act", lambda E, jj=jj, pz=pz: E.activation(out=stg["z"][:, jj * TT:(jj + 1) * TT], in_=pz[:], func=AF.Silu), reads=[pzb], writes=[stgb["z"]])
                    P.op("pool", lambda E, ti=ti: E.tensor_tensor(out=tE[ti][:], in0=tA[ti][:], in1=tA[ti][:], op=ALU.mult), reads=[tAb[ti]], writes=[tEb[ti]])
                    P.op("pool", lambda E, ti=ti: E.tensor_tensor(out=tF[ti][:], in0=tB[ti][:], in1=tB[ti][:], op=ALU.mult), reads=[tBb[ti]], writes=[tFb[ti]])

                    def dfn(ti=ti, jj=jj):
                        psq, psqb = banks.get()
                        P.group([lambda E: E.matmul(psq[:], C.blk1[:], tE[ti][:], start=True, stop=True)], reads=[tEb[ti], C.cb], writes=[psqb])
                        psk, pskb = banks.get()
                        P.group([lambda E: E.matmul(psk[:], C.blk1[:], tF[ti][:], start=True, stop=True)], reads=[tFb[ti]], writes=[pskb])
                        P.op("act", lambda E: E.activation(out=tC[ti][:], in_=psq[:], func=AF.Sqrt, bias=64 * EPS, scale=1.0), reads=[psqb], writes=[tCb[ti]])
                        P.op("act", lambda E: E.activation(out=tD[ti][:], in_=psk[:], func=AF.Sqrt, bias=EPS, scale=1.0 / 64), reads=[pskb], writes=[tDb[ti]])
                        P.op("dve", lambda E: E.reciprocal(out=tC[ti][:], in_=tC[ti][:]), writes=[tCb[ti]])
                        P.op("dve", lambda E: E.reciprocal(out=tD[ti][:], in_=tD[ti][:]), writes=[tDb[ti]])
                        P.op("dve", lambda E: E.scalar_tensor_tensor(out=stg["q"][:, jj * TT:(jj + 1) * TT], in0=tA[ti][:], scalar=par[:, PC_QG + j:PC_QG + j + 1],
                                                                     in1=tC[ti][:], op0=ALU.mult, op1=ALU.mult),
                             reads=[tAb[ti], tCb[ti]], writes=[stgb["q"]])
                        P.op("dve", lambda E: E.scalar_tensor_tensor(out=stg["k"][:, jj * TT:(jj + 1) * TT], in0=tB[ti][:], scalar=par[:, PC_KG + j:PC_KG + j + 1],
                                                                     in1=tD[ti][:], op0=ALU.mult, op1=ALU.mult),
                             reads=[tBb[ti], tDb[ti]], writes=[stgb["k"]])
                    deferred = dfn
            deferred()
            tsl = slice(i * TT, (i + 1) * TT)
            dst = {"y": C.yT[0:1024, tsl], "q": C.t1[0:1024, tsl], "k": C.t1[1024:2048, tsl], "v": C.t2[0:1024, tsl], "z": C.t2[1024:2048, tsl]}
            for k in ("y", "q", "k", "v", "z"):
                P.dma("pool", stgb[k].sem, dst[k].rearrange("(c p) t -> p c t", p=128), stg[k][:].rearrange("p (c t) -> p c t", c=8),
                      reads=[stgb[k]], dram_write=True)
        P.barrier()
        P.emit()
        for b in xtb + wtb + list(stgb.values()):
            P.put_sem(b.sem)


def phaseB_even(C, l):
    nc, P = C.nc, C.P
    SB = 2048
    with ExitStack() as st:
        sb = lambda n, s, d: st.enter_context(nc.sbuf_tensor(U(n), s, d))
        kn = [sb("kn%d" % i, [128, SB], BF16) for i in range(3)]; knb = [Buf("kn%d" % i, P.get_sem()) for i in range(3)]
        vv = [sb("vv%d" % i, [128, SB], BF16) for i in range(3)]; vvb = [Buf("vv%d" % i, P.get_sem()) for i in range(3)]
        qn = [sb("qn%d" % i, [128, SB], BF16) for i in range(2)]; qnb = [Buf("qn%d" % i, P.get_sem()) for i in range(2)]
        sz = [sb("sz%d" % i, [128, SB], BF16) for i in range(2)]; szb = [Buf("sz%d" % i, P.get_sem()) for i in range(2)]
        yo = [sb("yo%d" % i, [128, SB], BF16) for i in range(2)]; yob = [Buf("yo%d" % i, P.get_sem()) for i in range(2)]
        bia = sb("bia", [128, 2 * 768], F32); biab = Buf("bia", P.get_sem())
        msk = sb("msk", [128, 2 * 768], BF16); mskb = Buf("msk")
        numa = sb("numa", [128, SB], F32); numab = Buf("numa")
        dena = sb("dena", [128, SB], F32); denab = Buf("dena")
        NE = 3
        ee = [[sb("ee%d_%d" % (i, h), [128, 256], BF16) for h in range(2)] for i in range(NE)]
        eeb = [[Buf("ee") for h in range(2)] for i in range(NE)]
        pm = [[sb("pm%d_%d" % (i, h), [128, 256], BF16) for h in range(2)] for i in range(NE)]
        pmb = [[Buf("pm") for h in range(2)] for i in range(NE)]
        VA = [sb("VA%d" % i, [128, 2, 128], BF16) for i in range(NE)]; VAb = [Buf("VA") for i in range(NE)]
        VB = [sb("VB%d" % i, [128, 2, 128], BF16) for i in range(NE)]; VBb = [Buf("VB") for i in range(NE)]
        for i in range(NE):
            P.op("pool", lambda E, i=i: E.memset(VA[i][:], 0.0), writes=[VAb[i]])
            P.op("pool", lambda E, i=i: E.memset(VB[i][:], 0.0), writes=[VBb[i]])
        P._waits("sp", P.dram_pending)
        items = [(jp, s) for jp in range(8) for s in range(4)]

        def loads(idx):
            jp, s = items[idx]
            r = slice(jp * 128, (jp + 1) * 128)
            tsl = slice(s * SB, (s + 1) * SB)
            P.load(knb[idx % 3], kn[idx % 3][:], C.t1[1024 + jp * 128:1024 + (jp + 1) * 128, tsl])
            P.load(vvb[idx % 3], vv[idx % 3][:], C.t2[r, tsl])
            P.load(qnb[idx % 2], qn[idx % 2][:], C.t1[r, tsl])
            P.load(szb[idx % 2], sz[idx % 2][:], C.t2[1024 + jp * 128:1024 + (jp + 1) * 128, tsl])

        loads(0)
        nu = 0
        for idx, (jp, s) in enumerate(items):
            if idx + 1 < len(items):
                loads(idx + 1)
            if s == 0:
                P.load(biab, bia[:].rearrange("p (h c) -> p h c", h=2), C.biasT[2 * jp:2 * jp + 2].rearrange("h p c -> p h c"))
                P.op("act", lambda E: E.activation(out=msk[:], in_=bia[:], func=AF.Exp), reads=[biab], writes=[mskb])
            kc, kcb = kn[idx % 3], knb[idx % 3]
            kp, kpb = kn[(idx - 1) % 3], knb[(idx - 1) % 3]
            vc, vcb = vv[idx % 3], vvb[idx % 3]
            vp, vpb = vv[(idx - 1) % 3], vvb[(idx - 1) % 3]
            q, qb = qn[idx % 2], qnb[idx % 2]
            for pi, d in enumerate((1, 4, 16)):
                for u4 in range(4):
                    for uu in range(4):
                        if d == 1:
                            base = 512 * u4 + 128 * uu
                        elif d == 4:
                            base = 512 * u4 + uu
                        else:
                            base = 4 * u4 + uu
                        pbase = base - 128 * d
                        has_prev = (s > 0) or (pbase >= 0)
                        cur = slice(base, base + 127 * d + 1, d)
                        if pbase >= 0:
                            ksrc, ksrcb, vsrc, vsrcb, psl = kc, kcb, vc, vcb, slice(pbase, pbase + 127 * d + 1, d)
                        else:
                            ksrc, ksrcb, vsrc, vsrcb, psl = kp, kpb, vp, vpb, slice(pbase + SB, pbase + SB + 127 * d + 1, d)
                        e = nu % NE
                        sbank = [(C.pf[(nu % 2) * 2 + h], C.pfb[(nu % 2) * 2 + h]) for h in range(2)]
                        vpt, vptb = C.pb[nu % 2], C.pbb[nu % 2]
                        nu += 1
                        blks = ([0] if has_prev else []) + [1]
                        for h in range(2):
                            hs = slice(h * 64, (h + 1) * 64)
                            fns = []
                            for b in blks:
                                if b == 0:
                                    fns.append(lambda E, h=h, hs=hs, ksrc=ksrc, psl=psl, sbank=sbank, q=q, cur=cur: E.matmul(sbank[h][0][:, 0:128], ksrc[hs, psl], q[hs, cur], start=True, stop=True))
                                else:
                                    fns.append(lambda E, h=h, hs=hs, kc=kc, sbank=sbank, q=q, cur=cur: E.matmul(sbank[h][0][:, 128:256], kc[hs, cur], q[hs, cur], start=True, stop=True))
                            P.group(fns, reads=[kcb, qb] + ([ksrcb] if has_prev else []), writes=[sbank[h][1]])
                        fns = []
                        for b in blks:
                            if b == 0:
                                fns.append(lambda E, vpt=vpt, vsrc=vsrc, psl=psl: E.transpose(vpt[:, 0:128], vsrc[:, psl], C.ident[:]))
                            else:
                                fns.append(lambda E, vpt=vpt, vc=vc, cur=cur: E.transpose(vpt[:, 128:256], vc[:, cur], C.ident[:]))
                        P.group(fns, reads=[vcb] + ([vsrcb] if has_prev else []), writes=[vptb])
                        c0 = 0 if has_prev else 128
                        for h in range(2):
                            P.op("act", lambda E, e=e, h=h, c0=c0, sbank=sbank: E.activation(out=ee[e][h][:, c0:256], in_=sbank[h][0][:, c0:256], func=AF.Exp),
                                 reads=[sbank[h][1]], writes=[eeb[e][h]])
                            P.op("pool", lambda E, e=e, h=h, c0=c0, pi=pi: E.tensor_tensor(out=pm[e][h][:, c0:256], in0=ee[e][h][:, c0:256],
                                                                                         in1=msk[:, h * 768 + pi * 256 + c0:h * 768 + pi * 256 + 256], op=ALU.mult),
                                 reads=[eeb[e][h], mskb], writes=[pmb[e][h]])
                        b0 = blks[0]
                        vpt3 = vpt[:, 0:256].rearrange("p (b c) -> p b c", b=2)
                        P.op("dve", lambda E, e=e, b0=b0, vpt3=vpt3: E.tensor_copy(out=VA[e][:, b0:2, :], in_=vpt3[:, b0:2, :]), reads=[vptb], writes=[VAb[e]])
                        osl = slice(uu * 128, (uu + 1) * 128)
                        for h in range(2):
                            hs = slice(h * 64, (h + 1) * 64)
                            fn_n = [(lambda E, e=e, b=b, h=h, hs=hs, osl=osl, bi=bi, nb=len(blks): E.matmul(C.pf[4][hs, osl], VA[e][:, b, hs], pm[e][h][:, b * 128:(b + 1) * 128], start=(bi == 0), stop=(bi == nb - 1))) for bi, b in enumerate(blks)]
                            P.group(fn_n, reads=[VAb[e], pmb[e][h]], writes=[C.pfb[4]])
                            fn_d = [(lambda E, e=e, b=b, h=h, hs=hs, osl=osl, bi=bi, nb=len(blks): E.matmul(C.pf[5][hs, osl], C.onA[:, 0:64], pm[e][h][:, b * 128:(b + 1) * 128], start=(bi == 0), stop=(bi == nb - 1))) for bi, b in enumerate(blks)]
                            P.group(fn_d, reads=[pmb[e][h], C.cb], writes=[C.pfb[5]])
                    if d == 1:
                        nv = numa[:, 512 * u4:512 * (u4 + 1)]
                        dv = dena[:, 512 * u4:512 * (u4 + 1)]
                        P.op("act", lambda E, nv=nv: E.copy(out=nv, in_=C.pf[4][:]), reads=[C.pfb[4]], writes=[numab])
                        P.op("dve", lambda E, dv=dv: E.tensor_copy(out=dv, in_=C.pf[5][:]), reads=[C.pfb[5]], writes=[denab])
                    else:
                        if d == 4:
                            nv = numa[:, 512 * u4:512 * (u4 + 1)].rearrange("p (i r) -> p r i", r=4)
                            dv = dena[:, 512 * u4:512 * (u4 + 1)].rearrange("p (i r) -> p r i", r=4)
                        else:
                            nv = numa[:].rearrange("p (i r) -> p r i", r=16)[:, 4 * u4:4 * u4 + 4, :]
                            dv = dena[:].rearrange("p (i r) -> p r i", r=16)[:, 4 * u4:4 * u4 + 4, :]
                        p4 = C.pf[4][:].rearrange("p (r i) -> p r i", r=4)
                        p5 = C.pf[5][:].rearrange("p (r i) -> p r i", r=4)
                        P.op("dve", lambda E, nv=nv, p4=p4: E.tensor_tensor(out=nv, in0=p4, in1=nv, op=ALU.add), reads=[C.pfb[4]], writes=[numab])
                        P.op("dve", lambda E, dv=dv, p5=p5: E.tensor_tensor(out=dv, in0=p5, in1=dv, op=ALU.add), reads=[C.pfb[5]], writes=[denab])
            yb = idx % 2
            P.op("dve", lambda E: E.reciprocal(out=dena[:], in_=dena[:]), writes=[denab])
            P.op("dve", lambda E: E.tensor_tensor(out=numa[:], in0=numa[:], in1=dena[:], op=ALU.mult), reads=[denab], writes=[numab])
            P.op("pool", lambda E, yb=yb, idx=idx: E.tensor_tensor(out=yo[yb][:], in0=numa[:], in1=sz[idx % 2][:], op=ALU.mult),
                 reads=[numab, szb[idx % 2]], writes=[yob[yb]])
            P.dma("pool", yob[yb].sem, C.yT[1024 + jp * 128:1024 + (jp + 1) * 128, s * SB:(s + 1) * SB], yo[yb][:], reads=[yob[yb]], dram_write=True)
        P.barrier()
        P.emit()
        for b in knb + vvb + qnb + szb + yob + [biab]:
            P.put_sem(b.sem)


def phaseA_odd(C, l, xin):
    nc, P = C.nc, C.P
    j = l // 2
    par = C.par
    with ExitStack() as st:
        sb = lambda n, s, d: st.enter_context(nc.sbuf_tensor(U(n), s, d))
        xt = [sb("xt%d" % i, [128, 8 * TT], F32) for i in range(2)]
        xtb = [Buf("xt%d" % i, P.get_sem()) for i in range(2)]
        sq = sb("sq", [128, 8 * TT], BF16); sqb = Buf("sq")
        hn = [sb("hn%d" % i, [128, 8 * TT], BF16) for i in range(2)]
        hnb = [Buf("hn%d" % i) for i in range(2)]
        rs = sb("rs", [128, TT], F32); rsb = Buf("rs")
        wt = [sb("wt%d" % i, [128, 8 * 512], BF16) for i in range(3)]
        wtb = [Buf("wt%d" % i, P.get_sem()) for i in range(3)]
        sq_s = sb("stg_q", [128, 16 * TT], BF16); sq_b = Buf("stg_q", P.get_sem())
        sz_s = sb("stg_z", [128, 16 * TT], BF16); sz_b = Buf("stg_z", P.get_sem())
        sf_s = sb("stg_f", [128, 16 * TT], F32); sf_b = Buf("stg_f", P.get_sem())
        si_s = sb("stg_i", [128, 4 * 2048], BF16); si_b = Buf("stg_i", P.get_sem())
        NTMP = 3
        tA = [sb("tA%d" % i, [128, TT], F32) for i in range(NTMP)]; tAb = [Buf("tA") for i in range(NTMP)]
        tB = [sb("tB%d" % i, [128, TT], F32) for i in range(NTMP)]; tBb = [Buf("tB") for i in range(NTMP)]
        tC = [sb("tC%d" % i, [128, TT], F32) for i in range(NTMP)]; tCb = [Buf("tC") for i in range(NTMP)]
        banks = BankPool(C, range(6))
        P._waits("sp", P.dram_pending)

        def xload(i):
            P.load(xtb[i % 2], xt[i % 2][:].rearrange("p (k t) -> p k t", k=8),
                   xin[:, i * TT:(i + 1) * TT].rearrange("(k p) t -> p k t", p=128))

        nw = [0]

        def wload(g):
            s = nw[0] % 3
            nw[0] += 1
            P.load(wtb[s], wt[s][:].rearrange("p (k c) -> p k c", k=8),
                   C.wbi[l][:, g * 512:(g + 1) * 512].rearrange("(k p) c -> p k c", p=128))
            return s

        xload(0)
        wq = [wload(0), wload(1)]
        nt = 0
        for i in range(NT):
            if i + 1 < NT:
                xload(i + 1)
            hb = i % 2
            emit_norm(C, l, xt[i % 2], xtb[i % 2], sq, sqb, hn[hb], hnb[hb], rs, rsb, banks)
            hn3 = hn[hb][:].rearrange("p (k t) -> p k t", k=8)
            for g in range(16):
                s = wq.pop(0)
                gi = i * 16 + g + 2
                if gi < NT * 16:
                    wq.append(wload(gi % 16))
                w3 = wt[s][:].rearrange("p (k c) -> p k c", k=8)
                if g < 12:
                    for c in range(4):
                        cc = g * 4 + c
                        h, typ = cc // 3, cc % 3
                        ps, psb = banks.get()
                        P.group([(lambda E, k=k, c=c, ps=ps: E.matmul(ps[:], w3[:, k, c * 128:(c + 1) * 128], hn3[:, k, :], start=(k == 0), stop=(k == 7))) for k in range(8)],
                                reads=[wtb[s], hnb[hb]], writes=[psb])
                        hsl = slice(h * TT, (h + 1) * TT)
                        if typ == 0:
                            P.op("act", lambda E, ps=ps, hsl=hsl: E.activation(out=sq_s[:, hsl], in_=ps[:], func=AF.Silu), reads=[psb], writes=[sq_b])
                        elif typ == 2:
                            P.op("act", lambda E, ps=ps, hsl=hsl: E.activation(out=sz_s[:, hsl], in_=ps[:], func=AF.Silu), reads=[psb], writes=[sz_b])
                        else:
                            ti = nt % NTMP
                            nt += 1
                            P.op("act", lambda E, ps=ps, ti=ti: E.activation(out=tA[ti][:], in_=ps[:], func=AF.Exp, scale=-1.0), reads=[psb], writes=[tAb[ti]])
                            P.op("act", lambda E, ti=ti, h=h: E.activation(out=tB[ti][:], in_=tA[ti][:], func=AF.Ln, bias=1.0, scale=C.lbt[:, j, h:h + 1]),
                                 reads=[tAb[ti], C.cb], writes=[tBb[ti]])
                            P.op("act", lambda E, ti=ti: E.activation(out=tC[ti][:], in_=tA[ti][:], func=AF.Ln, bias=1.0, scale=1.0), reads=[tAb[ti]], writes=[tCb[ti]])
                            P.op("dve", lambda E, ti=ti, hsl=hsl: E.tensor_tensor(out=sf_s[:, hsl], in0=tB[ti][:], in1=tC[ti][:], op=ALU.subtract),
                                 reads=[tBb[ti], tCb[ti]], writes=[sf_b])
                else:
                    cb_ = g - 12
                    for tb in range(4):
                        ps, psb = banks.get()
                        P.group([(lambda E, k=k, tb=tb, ps=ps: E.matmul(ps[:], hn3[:, k, tb * 128:(tb + 1) * 128], w3[:, k, :], start=(k == 0), stop=(k == 7))) for k in range(8)],
                                reads=[wtb[s], hnb[hb]], writes=[psb])
                        osl = slice(tb * 2048 + cb_ * 512, tb * 2048 + (cb_ + 1) * 512)
                        if tb % 2 == 0:
                            P.op("dve", lambda E, ps=ps, osl=osl: E.tensor_copy(out=si_s[:, osl], in_=ps[:]), reads=[psb], writes=[si_b])
                        else:
                            P.op("act", lambda E, ps=ps, osl=osl: E.copy(out=si_s[:, osl], in_=ps[:]), reads=[psb], writes=[si_b])
            tsl = slice(i * TT, (i + 1) * TT)
            P.dma("pool", sq_b.sem, C.t1[:, tsl].rearrange("(c p) t -> p c t", p=128), sq_s[:].rearrange("p (c t) -> p c t", c=16), reads=[sq_b], dram_write=True)
            P.dma("pool", sz_b.sem, C.t2[:, tsl].rearrange("(c p) t -> p c t", p=128), sz_s[:].rearrange("p (c t) -> p c t", c=16), reads=[sz_b], dram_write=True)
            P.dma("pool", sf_b.sem, C.t3[:, tsl].rearrange("(c p) t -> p c t", p=128), sf_s[:].rearrange("p (c t) -> p c t", c=16), reads=[sf_b], dram_write=True)
            P.dma("pool", si_b.sem, C.t4[tsl, :].rearrange("(b p) n -> p b n", p=128), si_s[:].rearrange("p (b n) -> p b n", b=4), reads=[si_b], dram_write=True)
        P.barrier()
        P.emit()
        for b in xtb + wtb + [sq_b, sz_b, sf_b, si_b]:
            P.put_sem(b.sem)


def phaseB_odd(C, l):
    nc, P = C.nc, C.P
    j = l // 2
    par = C.par
    QT = 2048
    with ExitStack() as st:
        sb = lambda n, s, d: st.enter_context(nc.sbuf_tensor(U(n), s, d))
        qs = [sb("qs%d" % i, [128, QT], BF16) for i in range(2)]; qsb = [Buf("qs", P.get_sem()) for i in range(2)]
        lf = [sb("lf%d" % i, [128, QT], F32) for i in range(2)]; lfb = [Buf("lf", P.get_sem()) for i in range(2)]
        vt = [sb("vt%d" % i, [128, 16, 128], BF16) for i in range(2)]; vtb = [Buf("vt", P.get_sem()) for i in range(2)]
        sz = [sb("sz%d" % i, [128, QT], BF16) for i in range(2)]; szb = [Buf("sz", P.get_sem()) for i in range(2)]
        yo = [sb("yo%d" % i, [128, QT], BF16) for i in range(2)]; yob = [Buf("yo", P.get_sem()) for i in range(2)]
        NS = 2
        gc = [sb("gc%d" % i, [128, TT], F32) for i in range(NS)]; gcb = [Buf("gc") for i in range(NS)]
        eg = [sb("eg%d" % i, [128, TT], F32) for i in range(NS)]; egb = [Buf("eg") for i in range(NS)]
        en = [sb("en%d" % i, [128, TT], F32) for i in range(NS)]; enb = [Buf("en") for i in range(NS)]
        kk = [sb("kk%d" % i, [128, TT], F32) for i in range(NS)]; kkb = [Buf("kk") for i in range(NS)]
        qt = [sb("qt%d" % i, [128, TT], BF16) for i in range(NS)]; qtb = [Buf("qt") for i in range(NS)]
        kt = [sb("kt%d" % i, [128, TT], BF16) for i in range(NS)]; ktb = [Buf("kt") for i in range(NS)]
        kh = [sb("kh%d" % i, [128, TT], BF16) for i in range(NS)]; khb = [Buf("kh") for i in range(NS)]
        km = [sb("km%d" % i, [128, 128], BF16) for i in range(3)]; kmb = [Buf("km") for i in range(3)]
        at = [sb("at%d" % i, [128, 64], BF16) for i in range(4)]; atb = [Buf("at") for i in range(4)]
        Sf = sb("Sf", [128, 128], F32); Sfb = Buf("Sf")
        Sb_ = [sb("Sb%d" % i, [128, 128], BF16) for i in range(4)]; Sbb = [Buf("Sb") for i in range(4)]
        osb = sb("osb", [128, TT], F32); osbb = Buf("osb")
        osq = sb("osq", [128, TT], BF16); osqb = Buf("osq")
        rr = sb("rr", [128, TT], F32); rrb = Buf("rr")
        P._waits("sp", P.dram_pending)
        items = [(h, qd) for h in range(16) for qd in range(4)]

        def loads(idx):
            h, qd = items[idx]
            r = slice(h * 128, (h + 1) * 128)
            tsl = slice(qd * QT, (qd + 1) * QT)
            P.load(qsb[idx % 2], qs[idx % 2][:], C.t1[r, tsl])
            P.load(lfb[idx % 2], lf[idx % 2][:], C.t3[r, tsl])
            P.load(vtb[idx % 2], vt[idx % 2][:], C.t4[tsl, r].rearrange("(b p) n -> p b n", p=128))
            P.load(szb[idx % 2], sz[idx % 2][:], C.t2[r, tsl])

        loads(0)
        nseg = 0
        nch = 0
        npair = 0
        for idx, (h, qd) in enumerate(items):
            if idx + 1 < len(items):
                loads(idx + 1)
            ib = idx % 2
            if qd == 0:
                P.op("dve", lambda E: E.memset(Sf[:], 0.0), writes=[Sfb])
                sbi = nch % 4
                P.op("pool", lambda E, sbi=sbi: E.memset(Sb_[sbi][:], 0.0), writes=[Sbb[sbi]])
            for sg in range(QT // TT):
                ss = nseg % NS
                nseg += 1
                seg = slice(sg * TT, (sg + 1) * TT)
                for c in range(8):
                    cs = slice(c * 64, (c + 1) * 64)
                    P.op("dve", lambda E, ss=ss, cs=cs, c=c, ib=ib, sg=sg: E.tensor_tensor_scan(out=gc[ss][:, cs], data0=C.ones_f[:], data1=lf[ib][:, sg * TT + c * 64:sg * TT + (c + 1) * 64],
                                                                                              initial=0.0, op0=ALU.mult, op1=ALU.add),
                         reads=[lfb[ib], C.cb], writes=[gcb[ss]])
                P.op("act", lambda E, ss=ss: E.activation(out=eg[ss][:], in_=gc[ss][:], func=AF.Exp), reads=[gcb[ss]], writes=[egb[ss]])
                P.op("act", lambda E, ss=ss: E.activation(out=en[ss][:], in_=gc[ss][:], func=AF.Exp, scale=-1.0), reads=[gcb[ss]], writes=[enb[ss]])
                P.op("act", lambda E, ss=ss, ib=ib, seg=seg: E.activation(out=kk[ss][:], in_=lf[ib][:, seg], func=AF.Exp), reads=[lfb[ib]], writes=[kkb[ss]])
                P.op("pool", lambda E, ss=ss: E.tensor_scalar(out=kk[ss][:], in0=kk[ss][:], scalar1=-1.0, scalar2=1.0, op0=ALU.mult, op1=ALU.add), writes=[kkb[ss]])
                P.op("dve", lambda E, ss=ss, ib=ib, seg=seg: E.tensor_tensor(out=qt[ss][:], in0=qs[ib][:, seg], in1=eg[ss][:], op=ALU.mult), reads=[qsb[ib], egb[ss]], writes=[qtb[ss]])
                P.op("pool", lambda E, ss=ss: E.tensor_tensor(out=kt[ss][:], in0=kk[ss][:], in1=en[ss][:], op=ALU.mult), reads=[kkb[ss], enb[ss]], writes=[ktb[ss]])
                for c in range(8):
                    cs = slice(c * 64, (c + 1) * 64)
                    P.op("pool", lambda E, ss=ss, cs=cs, c=c: E.tensor_scalar(out=kh[ss][:, cs], in0=kt[ss][:, cs], scalar1=eg[ss][:, c * 64 + 63:c * 64 + 64], scalar2=None, op0=ALU.mult),
                         reads=[ktb[ss], egb[ss]], writes=[khb[ss]])
                for c in range(8):
                    par_ = c % 2
                    prs = slice(par_ * 64, (par_ + 1) * 64)
                    cs = slice(c * 64, (c + 1) * 64)
                    blk = (sg * TT + c * 64) // 128
                    if par_ == 0:
                        kmi = npair % 3
                        tp, tpb = C.pb[npair % 2], C.pbb[npair % 2]
                        npair += 1
                        P.group([lambda E, ss=ss, c=c, tp=tp: E.transpose(tp[:, 0:128], kh[ss][:, c * 64:c * 64 + 128], C.ident[:])], reads=[khb[ss], C.cb], writes=[tpb])
                        P.op("act", lambda E, kmi=kmi, tp=tp: E.copy(out=km[kmi][:], in_=tp[:, 0:128]), reads=[tpb], writes=[kmb[kmi]])
                    sc, scb = C.pf[nch % 2], C.pfb[nch % 2]
                    dS, dSb = C.pf[2 + nch % 2], C.pfb[2 + nch % 2]
                    ai = nch % 4
                    sbi = nch % 4
                    sbn = (nch + 1) % 4
                    nch += 1
                    P.group([lambda E, ss=ss, cs=cs, sc=sc, prs=prs: E.matmul(sc[prs, 0:64], kt[ss][:, cs], qt[ss][:, cs], start=True, stop=True)],
                            reads=[ktb[ss], qtb[ss]], writes=[scb])
                    P.op("dve", lambda E, ai=ai, sc=sc, prs=prs: E.tensor_tensor(out=at[ai][prs, :], in0=sc[prs, 0:64], in1=C.tri[prs, :], op=ALU.mult),
                         reads=[scb, C.cb], writes=[atb[ai]])
                    P.group([lambda E, kmi=kmi, prs=prs, dS=dS, ib=ib, blk=blk: E.matmul(dS[:, 0:128], km[kmi][prs, :], vt[ib][prs, blk, :], start=True, stop=True)],
                            reads=[kmb[kmi], vtb[ib]], writes=[dSb])
                    P.group([lambda E, ai=ai, prs=prs, ib=ib, blk=blk, cs=cs: E.matmul(C.pf[4][:, cs], vt[ib][prs, blk, :], at[ai][prs, :], start=True, stop=False),
                             lambda E, sbi=sbi, ss=ss, cs=cs: E.matmul(C.pf[4][:, cs], Sb_[sbi][:], qt[ss][:, cs], start=False, stop=True)],
                            reads=[vtb[ib], atb[ai], Sbb[sbi], qtb[ss]], writes=[C.pfb[4]])
                    P.op("dve", lambda E, ss=ss, c=c, dS=dS: E.scalar_tensor_tensor(out=Sf[:], in0=Sf[:], scalar=eg[ss][:, c * 64 + 63:c * 64 + 64], in1=dS[:, 0:128], op0=ALU.mult, op1=ALU.add),
                         reads=[dSb, egb[ss]], writes=[Sfb])
                    P.op("dve", lambda E, sbn=sbn: E.tensor_copy(out=Sb_[sbn][:], in_=Sf[:]), reads=[Sfb], writes=[Sbb[sbn]])
                P.op("act", lambda E: E.copy(out=osb[:], in_=C.pf[4][:]), reads=[C.pfb[4]], writes=[osbb])
                P.op("pool", lambda E: E.tensor_tensor(out=osq[:], in0=osb[:], in1=osb[:], op=ALU.mult), reads=[osbb], writes=[osqb])
                P.group([lambda E: E.matmul(C.pf[5][:], C.ones_h[:], osq[:], start=True, stop=True)], reads=[osqb, C.cb], writes=[C.pfb[5]])
                P.op("act", lambda E: E.activation(out=rr[:], in_=C.pf[5][:], func=AF.Sqrt, bias=EPS, scale=1.0), reads=[C.pfb[5]], writes=[rrb])
                P.op("dve", lambda E: E.reciprocal(out=rr[:], in_=rr[:]), writes=[rrb])
                P.op("dve", lambda E, h=h: E.scalar_tensor_tensor(out=osb[:], in0=osb[:], scalar=par[:, PC_OG + j * 16 + h:PC_OG + j * 16 + h + 1], in1=rr[:], op0=ALU.mult, op1=ALU.mult),
                     reads=[rrb, C.cb], writes=[osbb])
                P.op("pool", lambda E, ib=ib, seg=seg: E.tensor_tensor(out=yo[ib][:, seg], in0=osb[:], in1=sz[ib][:, seg], op=ALU.mult), reads=[osbb, szb[ib]], writes=[yob[ib]])
            P.dma("pool", yob[ib].sem, C.yT[h * 128:(h + 1) * 128, qd * QT:(qd + 1) * QT], yo[ib][:], reads=[yob[ib]], dram_write=True)
        P.barrier()
        P.emit()
        for b in qsb + lfb + vtb + szb + yob:
            P.put_sem(b.sem)


def phaseC(C, l, xin, xout):
    nc, P = C.nc, C.P
    with ExitStack() as st:
        sb = lambda n, s, d: st.enter_context(nc.sbuf_tensor(U(n), s, d))
        wo = sb("wo", [128, 16 * 1024], BF16); wob = Buf("wo", P.get_sem())
        yt = [sb("yt%d" % i, [128, 16 * TT], BF16) for i in range(2)]; ytb = [Buf("yt", P.get_sem()) for i in range(2)]
        xt = [sb("xt%d" % i, [128, 8 * TT], F32) for i in range(2)]; xtb = [Buf("xt", P.get_sem()) for i in range(2)]
        xo = [sb("xo%d" % i, [128, 8 * TT], F32) for i in range(2)]; xob = [Buf("xo", P.get_sem()) for i in range(2)]
        banks = BankPool(C, range(6))
        P._waits("sp", P.dram_pending)
        P.load(wob, wo[:].rearrange("p (k c) -> p k c", k=16), C.wbo[l].rearrange("(k p) c -> p k c", p=128))
        wo3 = wo[:].rearrange("p (k c) -> p k c", k=16)

        def loads(i):
            tsl = slice(i * TT, (i + 1) * TT)
            P.load(ytb[i % 2], yt[i % 2][:].rearrange("p (k t) -> p k t", k=16), C.yT[:, tsl].rearrange("(k p) t -> p k t", p=128))
            P.load(xtb[i % 2], xt[i % 2][:].rearrange("p (k t) -> p k t", k=8), xin[:, tsl].rearrange("(k p) t -> p k t", p=128))

        loads(0)
        for i in range(NT):
            if i + 1 < NT:
                loads(i + 1)
            b = i % 2
            y3 = yt[b][:].rearrange("p (k t) -> p k t", k=16)
            for oc in range(8):
                ps, psb = banks.get()
                P.group([(lambda E, k=k, oc=oc, ps=ps: E.matmul(ps[:], wo3[:, k, oc * 128:(oc + 1) * 128], y3[:, k, :], start=(k == 0), stop=(k == 15))) for k in range(16)],
                        reads=[wob, ytb[b]], writes=[psb])
                osl = slice(oc * TT, (oc + 1) * TT)
                P.op("dve", lambda E, ps=ps, osl=osl, b=b: E.tensor_tensor(out=xo[b][:, osl], in0=ps[:], in1=xt[b][:, osl], op=ALU.add),
                     reads=[psb, xtb[b]], writes=[xob[b]])
            P.dma("pool", xob[b].sem, xout[:, i * TT:(i + 1) * TT].rearrange("(k p) t -> p k t", p=128), xo[b][:].rearrange("p (k t) -> p k t", k=8),
                  reads=[xob[b]], dram_write=True)
        P.barrier()
        P.emit()
        for b_ in [wob] + ytb + xtb + xob:
            P.put_sem(b_.sem)


def _t5_bucket(distance):
    max_exact = 16
    dist = np.maximum(distance, max_exact).astype(np.float32)
    scaled = np.log(dist / np.float32(max_exact)) / np.float32(math.log(2048 / max_exact))
    large = np.minimum(max_exact + (scaled.astype(np.float32) * np.float32(16)).astype(np.int32), 31)
    return np.where(distance < max_exact, distance, large)


def _host_tables(inputs):
    f32 = np.float32
    par = np.zeros((128, NPAR), f32)
    for l in range(NL):
        ln = inputs["ln_even"][l // 2] if l % 2 == 0 else inputs["ln_odd"][l // 2]
        par[:, PC_LN + l * 8:PC_LN + (l + 1) * 8] = np.asarray(ln, f32).reshape(8, 128).T
    cw = np.asarray(inputs["conv_w"], f32)
    for j in range(2):
        for k in range(3):
            par[:, PC_CONV + (j * 3 + k) * 8:PC_CONV + (j * 3 + k + 1) * 8] = cw[j, k].reshape(8, 128).T
    for j in range(2):
        par[:, PC_QG + j] = np.tile(np.asarray(inputs["q_gain"], f32)[j], 2)
        par[:, PC_KG + j] = np.tile(np.asarray(inputs["k_gain"], f32)[j], 2)
        par[:, PC_LB + j * 16:PC_LB + (j + 1) * 16] = np.asarray(inputs["lower_bounds"], f32)[j].reshape(16, 128).T
        par[:, PC_OG + j * 16:PC_OG + (j + 1) * 16] = np.asarray(inputs["o_gain"], f32)[j].reshape(16, 128).T
    rb = np.asarray(inputs["rel_bias"], f32)
    kj = np.arange(128)[:, None]
    qi = np.arange(128)[None, :]
    bias = np.full((16, 128, 768), NEG, f32)
    for pi, d in enumerate((1, 4, 16)):
        for blk in range(2):
            delta = (128 if blk == 0 else 0) + qi - kj
            valid = (delta >= 0) & (delta <= 128)
            bucket = _t5_bucket(np.maximum(delta, 0) * d)
            tab = rb[bucket]
            tab = np.where(valid[:, :, None], tab, f32(NEG))
            bias[:, :, pi * 256 + blk * 128:pi * 256 + (blk + 1) * 128] = tab.transpose(2, 0, 1)
    tri = np.zeros((128, 64), f32)
    s_ = np.arange(128)[:, None] % 64
    t_ = np.arange(64)[None, :]
    tri[:] = (s_ <= t_).astype(f32)
    return par, bias, tri


_NC_CACHE = {}


def kernel(x, ln_even, w_in_even, conv_w, q_gain, k_gain, w_out_even, rel_bias,
           ln_odd, w_in_odd, lower_bounds, o_gain, w_out_odd, _nlayers=NL):
    inputs = dict(x=x, ln_even=ln_even, w_in_even=w_in_even, conv_w=conv_w, q_gain=q_gain, k_gain=k_gain,
                  w_out_even=w_out_even, rel_bias=rel_bias, ln_odd=ln_odd, w_in_odd=w_in_odd,
                  lower_bounds=lower_bounds, o_gain=o_gain, w_out_odd=w_out_odd)
    par, bias, tri = _host_tables(inputs)
    x = np.asarray(x, np.float32)
    B = x.shape[0]
    if _nlayers not in _NC_CACHE:
        _NC_CACHE[_nlayers] = build(_nlayers)
    nc = _NC_CACHE[_nlayers]
    common = dict(w_in_even=np.ascontiguousarray(w_in_even, np.float32), w_in_odd=np.ascontiguousarray(w_in_odd, np.float32),
                  w_out_even=np.ascontiguousarray(w_out_even, np.float32), w_out_odd=np.ascontiguousarray(w_out_odd, np.float32),
                  par=par, biasT=bias, tri=tri)
    in_maps = []
    for b in range(B):
        m = dict(common)
        m["xT"] = np.ascontiguousarray(x[b].T)
        in_maps.append(m)
    res = run_bass_kernel_spmd(nc, in_maps, core_ids=list(range(B)))
    out = np.stack([np.ascontiguousarray(r["outT"].T) for r in res.results], axis=0)
    return out.astype(np.float32)
```

```python
import math
from contextlib import ExitStack

import numpy as np
import concourse.bass as bass
import concourse.mybir as mybir
from concourse.bass_utils import run_bass_kernel_spmd

F32 = mybir.dt.float32
BF16 = mybir.dt.bfloat16
AF = mybir.ActivationFunctionType
ALU = mybir.AluOpType

D = 1024
S = 8192
NL = 4
EPS = 1e-6
TT = 512
NT = S // TT
NEG = -30000.0

PC_LN = 0
PC_CONV = 32
PC_QG = 80
PC_KG = 82
PC_LB = 84
PC_OG = 116
NPAR = 148


import types


def freeze(fn):
    if fn.__closure__ is None:
        return fn
    cells = []
    for c in fn.__closure__:
        try:
            cells.append(types.CellType(c.cell_contents))
        except ValueError:
            cells.append(c)
    return types.FunctionType(fn.__code__, fn.__globals__, fn.__name__, fn.__defaults__, tuple(cells))


class Buf:
    def __init__(self, name, sem=None):
        self.name = name
        self.w = None
        self.r = []
        self.sem = sem


class Eng:
    def __init__(self, name):
        self.name = name
        self.ops = []
        self.sem = None
        self.waited = {}


class Prog:
    def __init__(self, nc, stack):
        self.nc = nc
        self.engs = {n: Eng(n) for n in ("pe", "act", "dve", "pool", "sp")}
        self.sems = {}
        self.semcount = {}
        self.free_sems = []
        self.free_by_kind = {}
        self.nsem = 0
        for n in self.engs:
            k = "E_" + n
            self.sems[k] = stack.enter_context(nc.semaphore(k))
            self.semcount[k] = 0
            self.engs[n].sem = k
        self.stack = stack
        self.dram_pending = []

    def get_sem(self, kind="L"):
        fl = self.free_by_kind.setdefault(kind, [])
        if fl:
            return fl.pop()
        k = "D%s_%d" % (kind, self.nsem)
        self.nsem += 1
        self.sems[k] = self.stack.enter_context(self.nc.semaphore(k))
        self.semcount[k] = 0
        return k

    def put_sem(self, k):
        self.free_by_kind.setdefault(k[1], []).append(k)

    def _waits(self, eng, deps):
        e = self.engs[eng]
        for d in deps:
            if d is None:
                continue
            key, val = d
            if e.waited.get(key, 0) >= val:
                continue
            e.waited[key] = val
            h = self.sems[key]
            e.ops.append(lambda E, h=h, val=val: E.wait_ge(h, val))

    def _deps(self, reads, writes):
        deps = []
        for b in reads:
            deps.append(b.w)
        for b in writes:
            deps.append(b.w)
            deps.extend(b.r)
        return deps

    def _commit(self, tok, reads, writes):
        for b in reads:
            b.r.append(tok)
        for b in writes:
            b.w = tok
            b.r = []

    def op(self, eng, fn, reads=(), writes=()):
        fn = freeze(fn)
        self._waits(eng, self._deps(reads, writes))
        e = self.engs[eng]
        key = e.sem
        self.semcount[key] += 1
        tok = (key, self.semcount[key])
        h = self.sems[key]
        e.ops.append(lambda E, fn=fn, h=h: fn(E).then_inc(h, 1))
        self._commit(tok, reads, writes)
        return tok

    def group(self, fns, reads=(), writes=()):
        fns = [freeze(f) for f in fns]
        self._waits("pe", self._deps(reads, writes))
        e = self.engs["pe"]
        for fn in fns[:-1]:
            e.ops.append(lambda E, fn=fn: fn(E))
        key = e.sem
        self.semcount[key] += 1
        tok = (key, self.semcount[key])
        h = self.sems[key]
        e.ops.append(lambda E, fn=fns[-1], h=h: fn(E).then_inc(h, 1))
        self._commit(tok, reads, writes)
        return tok

    def dma(self, eng, sem, out, in_, reads=(), writes=(), dram_write=False):
        self._waits(eng, self._deps(reads, writes))
        e = self.engs[eng]
        self.semcount[sem] += 16
        tok = (sem, self.semcount[sem])
        h = self.sems[sem]
        e.ops.append(lambda E, out=out, in_=in_, h=h: E.dma_start(out=out, in_=in_).then_inc(h, 16))
        self._commit(tok, reads, writes)
        if dram_write:
            self.dram_pending.append(tok)
        return tok

    def load(self, buf, out, in_):
        return self.dma("sp", buf.sem, out, in_, writes=[buf])

    def barrier(self):
        toks = [(k, v) for k, v in self.semcount.items() if v > 0]
        for n in self.engs:
            self._waits(n, toks)
        self.dram_pending = []

    def emit(self):
        nc = self.nc
        engs = self.engs
        with nc.Block() as block:
            @block.tensor
            def _(E):
                for f in engs["pe"].ops:
                    f(E)

            @block.scalar
            def _(E):
                for f in engs["act"].ops:
                    f(E)

            @block.vector
            def _(E):
                for f in engs["dve"].ops:
                    f(E)

            @block.gpsimd
            def _(E):
                for f in engs["pool"].ops:
                    f(E)

            @block.sync
            def _(E):
                for f in engs["sp"].ops:
                    f(E)
        for e in engs.values():
            e.ops = []


def even_src_cols():
    src = []
    for j in range(8):
        src += [1024 + 128 * j, 2048 + 128 * j, 128 * j, 6144 + 128 * j]
    for j in range(8):
        src += [3072 + 128 * j, 4096 + 128 * j, 5120 + 128 * j, 7168 + 128 * j]
    return src


def odd_src_cols():
    src = []
    for h in range(16):
        src += [128 * h, 2048 + 128 * h, 6144 + 128 * h]
    for c in range(16):
        src.append(4096 + 128 * c)
    return src


class Ctx:
    pass


_UID = [0]


def U(n):
    _UID[0] += 1
    return "%s_u%d" % (n, _UID[0])


def build(nlayers=NL):
    nc = bass.Bass("TRN2", target_bir_lowering=False)
    C = Ctx()
    C.nc = nc
    xT = nc.dram_tensor("xT", [D, S], F32, kind="ExternalInput").ap()
    w_in_even = nc.dram_tensor("w_in_even", [2, D, 8192], F32, kind="ExternalInput").ap()
    w_in_odd = nc.dram_tensor("w_in_odd", [2, D, 8192], F32, kind="ExternalInput").ap()
    w_out_even = nc.dram_tensor("w_out_even", [2, 2048, D], F32, kind="ExternalInput").ap()
    w_out_odd = nc.dram_tensor("w_out_odd", [2, 2048, D], F32, kind="ExternalInput").ap()
    par_d = nc.dram_tensor("par", [128, NPAR], F32, kind="ExternalInput").ap()
    bias_d = nc.dram_tensor("biasT", [16, 128, 768], F32, kind="ExternalInput").ap()
    tri_d = nc.dram_tensor("tri", [128, 256], F32, kind="ExternalInput").ap()
    yT_out = nc.dram_tensor("outT", [D, S], F32, kind="ExternalOutput").ap()

    C.wbi = [nc.dram_tensor("wbi%d" % l, [D, 8192], BF16).ap() for l in range(nlayers)]
    C.wbo = [nc.dram_tensor("wbo%d" % l, [2048, D], BF16).ap() for l in range(nlayers)]
    xs = [nc.dram_tensor("xs%d" % i, [D, S], F32).ap() for i in range(2)]
    C.yT = nc.dram_tensor("yTs", [2048, S], BF16).ap()
    C.t1 = nc.dram_tensor("t1s", [2048, S], BF16).ap()
    C.t2 = nc.dram_tensor("t2s", [2048, S], BF16).ap()
    C.t3 = nc.dram_tensor("t3s", [2048, S], F32).ap()
    C.t4 = nc.dram_tensor("t4s", [S, 2048], BF16).ap()
    C.w_in = [w_in_even, w_in_odd]
    C.biasT = bias_d
    C.w_out = [w_out_even, w_out_odd]

    with ExitStack() as gs:
        P = Prog(nc, gs)
        C.P = P
        sbt = lambda st, n, s, d: st.enter_context(nc.sbuf_tensor(U(n), s, d))
        par = sbt(gs, "par", [128, NPAR], F32)
        C.par = par
        ones_n = sbt(gs, "ones_n", [128, 128], BF16)
        ones_h = sbt(gs, "ones_h", [128, 128], BF16)
        blk1 = sbt(gs, "blk1", [128, 128], BF16)
        onA = sbt(gs, "onA", [128, 128], BF16)
        onB = sbt(gs, "onB", [128, 128], BF16)
        ident = sbt(gs, "ident", [128, 128], BF16)
        identf = sbt(gs, "identf", [128, 128], F32)
        tri = sbt(gs, "tri", [128, 256], F32)
        ones_f = sbt(gs, "ones_f", [128, 64], F32)
        lbt = sbt(gs, "lbt", [128, 2, 16], F32)
        C.ones_n, C.ones_h, C.blk1, C.onA, C.onB, C.ident, C.tri, C.ones_f, C.lbt = ones_n, ones_h, blk1, onA, onB, ident, tri, ones_f, lbt
        cb = Buf("consts", P.get_sem())
        C.cb = cb
        P.load(cb, par[:], par_d[:, :])
        P.load(cb, tri[:], tri_d[:, :])
        P.op("pool", lambda E: E.memset(ones_n[:], 1.0 / 1024), writes=[cb])
        P.op("pool", lambda E: E.memset(ones_h[:], 1.0 / 128), writes=[cb])
        P.op("pool", lambda E: E.memset(ones_f[:], 1.0), writes=[cb])
        P.op("pool", lambda E: E.memset(blk1[:], 0.0), writes=[cb])
        P.op("pool", lambda E: E.memset(blk1[0:64, 0:64], 1.0), writes=[cb])
        P.op("pool", lambda E: E.memset(blk1[64:128, 64:128], 1.0), writes=[cb])
        P.op("pool", lambda E: E.memset(onA[:], 0.0), writes=[cb])
        P.op("pool", lambda E: E.memset(onA[:, 0:64], 1.0), writes=[cb])
        P.op("pool", lambda E: E.memset(onB[:], 0.0), writes=[cb])
        P.op("pool", lambda E: E.memset(onB[:, 64:128], 1.0), writes=[cb])
        P.op("pool", lambda E: E.memset(identf[:], 0.0), writes=[cb])
        P.op("pool", lambda E: E.affine_select(out=identf[:], in_=identf[:], pattern=[[-1, 128]],
                                               compare_op=ALU.not_equal, fill=1.0, base=0, channel_multiplier=1), writes=[cb])
        P.op("pool", lambda E: E.tensor_copy(out=ident[:], in_=identf[:]), writes=[cb])
        P.op("pool", lambda E: E.memset(lbt[:, 0, :], 0.0), writes=[cb])
        P.op("dve", lambda E: E.tensor_tensor(out=lbt[:, 1, :], in0=par[:, PC_LB:PC_LB + 16], in1=par[:, PC_LB + 16:PC_LB + 32], op=ALU.subtract), writes=[cb])
        P.op("act", lambda E: E.activation(out=lbt[:, 1, :], in_=lbt[:, 1, :], func=AF.Exp), writes=[cb])
        P.op("dve", lambda E: E.tensor_scalar(out=lbt[:, 1, :], in0=lbt[:, 1, :], scalar1=1.0, scalar2=None, op0=ALU.add), writes=[cb])
        P.op("dve", lambda E: E.reciprocal(out=lbt[:, 1, :], in_=lbt[:, 1, :]), writes=[cb])
        P.barrier()
        P.emit()

        for l in range(nlayers):
            precast(C, l)
        import os
        kstop = int(os.environ.get("KSTOP", "99"))
        for l in range(nlayers):
            xin = xT if l == 0 else xs[(l - 1) % 2]
            xout = yT_out if l == nlayers - 1 else xs[l % 2]
            last = (l == nlayers - 1)
            if last and kstop < 1:
                break
            if l % 2 == 0:
                phaseA_even(C, l, xin)
                if last and kstop < 2:
                    break
                phaseB_even(C, l)
            else:
                phaseA_odd(C, l, xin)
                if last and kstop < 2:
                    break
                phaseB_odd(C, l)
            if last and kstop < 3:
                break
            phaseC(C, l, xin, xout)
        P.barrier()
        P.emit()
    return nc


def precast(C, l):
    nc, P = C.nc, C.P
    j = l // 2
    src_cols = even_src_cols() if l % 2 == 0 else odd_src_cols()
    w_in = C.w_in[l % 2]
    w_out = C.w_out[l % 2]
    with ExitStack() as st:
        stg = [st.enter_context(nc.sbuf_tensor(U("pc_f"), [128, 4096], F32)) for i in range(2)]
        stb = [st.enter_context(nc.sbuf_tensor(U("pc_b"), [128, 4096], BF16)) for i in range(2)]
        fb = [Buf("pcf%d" % i, P.get_sem()) for i in range(2)]
        bb = [Buf("pcb%d" % i, P.get_sem("S")) for i in range(2)]
        engs = ["dve", "pool", "act"]
        n = 0
        def cast(n, s):
            e = engs[n % 3]
            if e == "act":
                P.op(e, lambda E: E.copy(out=stb[s][:], in_=stg[s][:]), reads=[fb[s]], writes=[bb[s]])
            else:
                P.op(e, lambda E: E.tensor_copy(out=stb[s][:], in_=stg[s][:]), reads=[fb[s]], writes=[bb[s]])

        for g in range(16):
            s = n % 2
            f3 = stg[s][:].rearrange("p (k c) -> p k c", k=8)
            for c in range(4):
                sc = src_cols[g * 4 + c]
                P.load(fb[s], f3[:, :, c * 128:(c + 1) * 128],
                       w_in[j, :, sc:sc + 128].rearrange("(k p) c -> p k c", p=128))
            cast(n, s)
            P.dma("pool", bb[s].sem, C.wbi[l][:, g * 512:(g + 1) * 512].rearrange("(k p) c -> p k c", p=128),
                  stb[s][:].rearrange("p (k c) -> p k c", k=8), reads=[bb[s]], dram_write=True)
            n += 1
        for g in range(4):
            s = n % 2
            P.load(fb[s], stg[s][:].rearrange("p (k c) -> p k c", k=16),
                   w_out[j, :, g * 256:(g + 1) * 256].rearrange("(k p) c -> p k c", p=128))
            cast(n, s)
            P.dma("pool", bb[s].sem, C.wbo[l][:, g * 256:(g + 1) * 256].rearrange("(k p) c -> p k c", p=128),
                  stb[s][:].rearrange("p (k c) -> p k c", k=16), reads=[bb[s]], dram_write=True)
            n += 1
        P.barrier()
        P.emit()
        for b in fb + bb:
            P.put_sem(b.sem)


class BankPool:
    def __init__(self, C, st, nbanks):
        self.t = [st.enter_context(C.nc.psum_tensor(U("bank"), [128, 512], F32)) for i in range(nbanks)]
        self.b = [Buf("bank") for i in range(nbanks)]
        self.n = 0

    def get(self):
        i = self.n % len(self.t)
        self.n += 1
        return self.t[i], self.b[i]


def emit_norm(C, l, xt, xtb, sq, sqb, hn, hnb, tmp, tmpb, banks):
    P = C.P
    par = C.par
    P.op("act", lambda E: E.activation(out=sq[:], in_=xt[:], func=AF.Square), reads=[xtb], writes=[sqb])
    ps, psb = banks.get()
    sq3 = sq[:].rearrange("p (k t) -> p k t", k=8)
    xt3 = xt[:].rearrange("p (k t) -> p k t", k=8)
    hn3 = hn[:].rearrange("p (k t) -> p k t", k=8)
    P.group([(lambda E, k=k: E.matmul(ps[:], C.ones_n[:], sq3[:, k, :], start=(k == 0), stop=(k == 7))) for k in range(8)],
            reads=[sqb, C.cb], writes=[psb])
    P.op("act", lambda E: E.activation(out=tmp[:], in_=ps[:], func=AF.Sqrt, bias=EPS, scale=1.0), reads=[psb], writes=[tmpb])
    P.op("dve", lambda E: E.reciprocal(out=tmp[:], in_=tmp[:]), writes=[tmpb])
    for k in range(8):
        P.op("dve", lambda E, k=k: E.scalar_tensor_tensor(out=hn3[:, k, :], in0=xt3[:, k, :], scalar=par[:, PC_LN + l * 8 + k:PC_LN + l * 8 + k + 1],
                                                          in1=tmp[:], op0=ALU.mult, op1=ALU.mult),
             reads=[xtb, tmpb, C.cb], writes=[hnb])


def phaseA_even(C, l, xin):
    nc, P = C.nc, C.P
    j = l // 2
    par = C.par
    with ExitStack() as st:
        sb = lambda n, s, d: st.enter_context(nc.sbuf_tensor(U(n), s, d))
        xt = [sb("xt%d" % i, [128, 8 * TT], F32) for i in range(2)]
        xtb = [Buf("xt%d" % i, P.get_sem()) for i in range(2)]
        sq = sb("sq", [128, 8 * TT], BF16); sqb = Buf("sq")
        hn = [sb("hn%d" % i, [128, 8 * TT], BF16) for i in range(2)]
        hnb = [Buf("hn%d" % i) for i in range(2)]
        rs = sb("rs", [128, TT], F32); rsb = Buf("rs")
        wt = [sb("wt%d" % i, [128, 8 * 512], BF16) for i in range(3)]
        wtb = [Buf("wt%d" % i, P.get_sem()) for i in range(3)]
        ubuf = [sb("ub%d" % i, [128, TT + 2], F32) for i in range(8)]
        ubb = [Buf("ub%d" % i) for i in range(8)]
        stg = {k: sb("stg_" + k, [128, 8 * TT], BF16) for k in ("y", "q", "k", "v", "z")}
        stgb = {k: Buf("stg_" + k, P.get_sem("S")) for k in stg}
        NTMP = 3
        tA = [sb("tA%d" % i, [128, TT], F32) for i in range(NTMP)]; tAb = [Buf("tA%d" % i) for i in range(NTMP)]
        tB = [sb("tB%d" % i, [128, TT], F32) for i in range(NTMP)]; tBb = [Buf("tB%d" % i) for i in range(NTMP)]
        tC = [sb("tC%d" % i, [128, TT], F32) for i in range(NTMP)]; tCb = [Buf("tC%d" % i) for i in range(NTMP)]
        tD = [sb("tD%d" % i, [128, TT], F32) for i in range(NTMP)]; tDb = [Buf("tD%d" % i) for i in range(NTMP)]
        tE = [sb("tE%d" % i, [128, TT], BF16) for i in range(NTMP)]; tEb = [Buf("tE%d" % i) for i in range(NTMP)]
        tF = [sb("tF%d" % i, [128, TT], BF16) for i in range(NTMP)]; tFb = [Buf("tF%d" % i) for i in range(NTMP)]
        banks = BankPool(C, st, 8)
        for i in range(8):
            P.op("pool", lambda E, i=i: E.memset(ubuf[i][:, 0:2], 0.0), writes=[ubb[i]])
        P.wait_dram = None
        P._waits("sp", P.dram_pending)

        def xload(i):
            P.load(xtb[i % 2], xt[i % 2][:].rearrange("p (k t) -> p k t", k=8),
                   xin[:, i * TT:(i + 1) * TT].rearrange("(k p) t -> p k t", p=128))

        nw = [0]

        def wload(g):
            s = nw[0] % 3
            nw[0] += 1
            P.load(wtb[s], wt[s][:].rearrange("p (k c) -> p k c", k=8),
                   C.wbi[l][:, g * 512:(g + 1) * 512].rearrange("(k p) c -> p k c", p=128))
            return s

        xload(0)
        wq = [wload(0), wload(1)]
        nt = [0]
        for i in range(NT):
            if i + 1 < NT:
                xload(i + 1)
            hb = i % 2
            emit_norm(C, l, xt[i % 2], xtb[i % 2], sq, sqb, hn[hb], hnb[hb], rs, rsb, banks)
            hn3 = hn[hb][:].rearrange("p (k t) -> p k t", k=8)
            deferred = None
            for g in range(16):
                s = wq.pop(0)
                gi = i * 16 + g + 2
                if gi < NT * 16:
                    wq.append(wload(gi % 16))
                w3 = wt[s][:].rearrange("p (k c) -> p k c", k=8)
                pcs = []
                for c in range(4):
                    ps, psb = banks.get()
                    P.group([(lambda E, k=k, c=c, ps=ps: E.matmul(ps[:], w3[:, k, c * 128:(c + 1) * 128], hn3[:, k, :], start=(k == 0), stop=(k == 7))) for k in range(8)],
                            reads=[wtb[s], hnb[hb]], writes=[psb])
                    pcs.append((ps, psb))
                if deferred is not None:
                    deferred()
                    deferred = None
                ti = nt[0] % NTMP
                nt[0] += 1
                if g < 8:
                    jj = g
                    (pgc, pgcb), (pxa, pxab), (pgb, pgbb), (pz, pzb) = pcs
                    ub, ubB = ubuf[jj], ubb[jj]
                    P.op("act", lambda E, ti=ti, pxa=pxa: E.copy(out=tA[ti][:], in_=pxa[:]), reads=[pxab], writes=[tAb[ti]])
                    P.op("dve", lambda E, ti=ti, pgc=pgc, ub=ub: E.tensor_tensor(out=ub[:, 2:TT + 2], in0=pgc[:], in1=tA[ti][:], op=ALU.mult),
                         reads=[pgcb, tAb[ti]], writes=[ubB])
                    cws = [par[:, PC_CONV + (j * 3 + k) * 8 + jj:PC_CONV + (j * 3 + k) * 8 + jj + 1] for k in range(3)]
                    cw = cws.__getitem__
                    P.op("dve", lambda E, ti=ti, ub=ub, cw=cw: E.tensor_scalar(out=tB[ti][:], in0=ub[:, 2:TT + 2], scalar1=cw(2), scalar2=None, op0=ALU.mult),
                         reads=[ubB, C.cb], writes=[tBb[ti]])
                    P.op("dve", lambda E, ti=ti, ub=ub, cw=cw: E.scalar_tensor_tensor(out=tB[ti][:], in0=ub[:, 1:TT + 1], scalar=cw(1), in1=tB[ti][:], op0=ALU.mult, op1=ALU.add),
                         reads=[ubB], writes=[tBb[ti]])
                    P.op("dve", lambda E, ti=ti, ub=ub, cw=cw: E.scalar_tensor_tensor(out=tB[ti][:], in0=ub[:, 0:TT], scalar=cw(0), in1=tB[ti][:], op0=ALU.mult, op1=ALU.add),
                         reads=[ubB], writes=[tBb[ti]])
                    P.op("act", lambda E, ti=ti, pz=pz: E.activation(out=tC[ti][:], in_=pz[:], func=AF.Silu), reads=[pzb], writes=[tCb[ti]])
                    P.op("dve", lambda E, ti=ti, pgb=pgb: E.tensor_tensor(out=tD[ti][:], in0=pgb[:], in1=tB[ti][:], op=ALU.mult),
                         reads=[pgbb, tBb[ti]], writes=[tDb[ti]])
                    P.op("pool", lambda E, ti=ti, jj=jj: E.tensor_tensor(out=stg["y"][:, jj * TT:(jj + 1) * TT], in0=tD[ti][:], in1=tC[ti][:], op=ALU.mult),
                         reads=[tDb[ti], tCb[ti]], writes=[stgb["y"]])
                    P.op("pool", lambda E, ub=ub: E.tensor_copy(out=ub[:, 0:2], in_=ub[:, TT:TT + 2]), writes=[ubB])
                else:
                    jj = g - 8
                    (pq, pqb), (pk, pkb), (pv, pvb), (pz, pzb) = pcs
                    P.op("act", lambda E, ti=ti, pq=pq: E.copy(out=tA[ti][:], in_=pq[:]), reads=[pqb], writes=[tAb[ti]])
                    P.op("act", lambda E, ti=ti, pk=pk: E.copy(out=tB[ti][:], in_=pk[:]), reads=[pkb], writes=[tBb[ti]])
                    P.op("act", lambda E, jj=jj, pv=pv: E.copy(out=stg["v"][:, jj * TT:(jj + 1) * TT], in_=pv[:]), reads=[pvb], writes=[stgb["v"]])
                    P.op("act", lambda E, jj=jj, pz=pz: E.activation(out=stg["z"][:, jj * TT:(jj + 1) * TT], in_=pz[:], func=AF.Silu), reads=[pzb], writes=[stgb["z"]])
                    P.op("pool", lambda E, ti=ti: E.tensor_tensor(out=tE[ti][:], in0=tA[ti][:], in1=tA[ti][:], op=ALU.mult), reads=[tAb[ti]], writes=[tEb[ti]])
                    P.op("pool", lambda E, ti=ti: E.tensor_tensor(out=tF[ti][:], in0=tB[ti][:], in1=tB[ti][:], op=ALU.mult), reads=[tBb[ti]], writes=[tFb[ti]])

                    def dfn(ti=ti, jj=jj):
                        psq, psqb = banks.get()
                        P.group([lambda E: E.matmul(psq[:], C.blk1[:], tE[ti][:], start=True, stop=True)], reads=[tEb[ti], C.cb], writes=[psqb])
                        psk, pskb = banks.get()
                        P.group([lambda E: E.matmul(psk[:], C.blk1[:], tF[ti][:], start=True, stop=True)], reads=[tFb[ti]], writes=[pskb])
                        P.op("act", lambda E: E.activation(out=tC[ti][:], in_=psq[:], func=AF.Sqrt, bias=64 * EPS, scale=1.0), reads=[psqb], writes=[tCb[ti]])
                        P.op("act", lambda E: E.activation(out=tD[ti][:], in_=psk[:], func=AF.Sqrt, bias=EPS, scale=1.0 / 64), reads=[pskb], writes=[tDb[ti]])
                        P.op("dve", lambda E: E.reciprocal(out=tC[ti][:], in_=tC[ti][:]), writes=[tCb[ti]])
                        P.op("dve", lambda E: E.reciprocal(out=tD[ti][:], in_=tD[ti][:]), writes=[tDb[ti]])
                        P.op("dve", lambda E: E.scalar_tensor_tensor(out=stg["q"][:, jj * TT:(jj + 1) * TT], in0=tA[ti][:], scalar=par[:, PC_QG + j:PC_QG + j + 1],
                                                                     in1=tC[ti][:], op0=ALU.mult, op1=ALU.mult),
                             reads=[tAb[ti], tCb[ti]], writes=[stgb["q"]])
                        P.op("dve", lambda E: E.scalar_tensor_tensor(out=stg["k"][:, jj * TT:(jj + 1) * TT], in0=tB[ti][:], scalar=par[:, PC_KG + j:PC_KG + j + 1],
                                                                     in1=tD[ti][:], op0=ALU.mult, op1=ALU.mult),
                             reads=[tBb[ti], tDb[ti]], writes=[stgb["k"]])
                    deferred = dfn
            deferred()
            tsl = slice(i * TT, (i + 1) * TT)
            dst = {"y": C.yT[0:1024, tsl], "q": C.t1[0:1024, tsl], "k": C.t1[1024:2048, tsl], "v": C.t2[0:1024, tsl], "z": C.t2[1024:2048, tsl]}
            for k in ("y", "q", "k", "v", "z"):
                P.dma("pool", stgb[k].sem, dst[k].rearrange("(c p) t -> p c t", p=128), stg[k][:].rearrange("p (c t) -> p c t", c=8),
                      reads=[stgb[k]], dram_write=True)
        P.barrier()
        P.emit()
        for b in xtb + wtb + list(stgb.values()):
            P.put_sem(b.sem)


def phaseB_even(C, l):
    nc, P = C.nc, C.P
    SB = 2048
    with ExitStack() as st:
        sb = lambda n, s, d: st.enter_context(nc.sbuf_tensor(U(n), s, d))
        kn = [sb("kn%d" % i, [128, SB], BF16) for i in range(3)]; knb = [Buf("kn%d" % i, P.get_sem()) for i in range(3)]
        vv = [sb("vv%d" % i, [128, SB], BF16) for i in range(3)]; vvb = [Buf("vv%d" % i, P.get_sem()) for i in range(3)]
        qn = [sb("qn%d" % i, [128, SB], BF16) for i in range(2)]; qnb = [Buf("qn%d" % i, P.get_sem()) for i in range(2)]
        sz = [sb("sz%d" % i, [128, SB], BF16) for i in range(2)]; szb = [Buf("sz%d" % i, P.get_sem()) for i in range(2)]
        yo = [sb("yo%d" % i, [128, SB], BF16) for i in range(2)]; yob = [Buf("yo%d" % i, P.get_sem("S")) for i in range(2)]
        bia = sb("bia", [128, 2 * 768], F32); biab = Buf("bia", P.get_sem())
        msk = sb("msk", [128, 2 * 768], BF16); mskb = Buf("msk")
        numa = sb("numa", [128, SB], F32); numab = Buf("numa")
        dena = sb("dena", [128, SB], F32); denab = Buf("dena")
        NE = 4
        ee = [sb("ee%d" % i, [128, 512], BF16) for i in range(NE)]; eeb = [Buf("ee") for i in range(NE)]
        pm = [sb("pm%d" % i, [128, 512], BF16) for i in range(NE)]; pmb = [Buf("pm") for i in range(NE)]
        VA = [sb("VA%d" % i, [128, 2, 128], BF16) for i in range(NE)]; VAb = [Buf("VA") for i in range(NE)]
        SP = [st.enter_context(nc.psum_tensor(U("SP"), [128, 1024], F32)) for i in range(2)]; SPb = [Buf("SP") for i in range(2)]
        VP = [st.enter_context(nc.psum_tensor(U("VP"), [128, 1024], BF16)) for i in range(2)]; VPb = [Buf("VP") for i in range(2)]
        pnum = st.enter_context(nc.psum_tensor(U("pnum"), [128, 512], F32)); pnumb = Buf("pnum")
        pden = st.enter_context(nc.psum_tensor(U("pden"), [128, 512], F32)); pdenb = Buf("pden")
        P._waits("sp", P.dram_pending)
        items = [(jp, s) for jp in range(8) for s in range(4)]

        def loads(idx):
            jp, s = items[idx]
            r = slice(jp * 128, (jp + 1) * 128)
            tsl = slice(s * SB, (s + 1) * SB)
            P.load(knb[idx % 3], kn[idx % 3][:], C.t1[1024 + jp * 128:1024 + (jp + 1) * 128, tsl])
            P.load(vvb[idx % 3], vv[idx % 3][:], C.t2[r, tsl])
            P.load(qnb[idx % 2], qn[idx % 2][:], C.t1[r, tsl])
            P.load(szb[idx % 2], sz[idx % 2][:], C.t2[1024 + jp * 128:1024 + (jp + 1) * 128, tsl])

        units = []
        for idx, (jp, s) in enumerate(items):
            for pi, d in enumerate((1, 4, 16)):
                for u4 in range(4):
                    for uu in range(4):
                        units.append((idx, jp, s, pi, d, u4, uu))

        def stage1(n):
            idx, jp, s, pi, d, u4, uu = units[n]
            kc, kcb = kn[idx % 3], knb[idx % 3]
            kp, kpb = kn[(idx - 1) % 3], knb[(idx - 1) % 3]
            vc, vcb = vv[idx % 3], vvb[idx % 3]
            vp, vpb = vv[(idx - 1) % 3], vvb[(idx - 1) % 3]
            q, qb = qn[idx % 2], qnb[idx % 2]
            if d == 1:
                base = 512 * u4 + 128 * uu
            elif d == 4:
                base = 512 * u4 + uu
            else:
                base = 4 * u4 + uu
            pbase = base - 128 * d
            has_prev = (s > 0) or (pbase >= 0)
            cur = slice(base, base + 127 * d + 1, d)
            if pbase >= 0:
                ksrc, ksrcb, vsrc, vsrcb, psl = kc, kcb, vc, vcb, slice(pbase, pbase + 127 * d + 1, d)
            else:
                ksrc, ksrcb, vsrc, vsrcb, psl = kp, kpb, vp, vpb, slice(pbase + SB, pbase + SB + 127 * d + 1, d)
            e = n % NE
            sp_, spb_ = SP[n % 2], SPb[n % 2]
            vpt, vptb = VP[n % 2], VPb[n % 2]
            blks = ([0] if has_prev else []) + [1]
            fns = []
            for h in range(2):
                hs = slice(h * 64, (h + 1) * 64)
                for b in blks:
                    if b == 0:
                        fns.append(lambda E, h=h, hs=hs: E.matmul(sp_[:, h * 512:h * 512 + 128], ksrc[hs, psl], q[hs, cur], start=True, stop=True))
                    else:
                        fns.append(lambda E, h=h, hs=hs: E.matmul(sp_[:, h * 512 + 128:h * 512 + 256], kc[hs, cur], q[hs, cur], start=True, stop=True))
            P.group(fns, reads=[kcb, qb] + ([ksrcb] if has_prev else []), writes=[spb_])
            fns = []
            for b in blks:
                if b == 0:
                    fns.append(lambda E: E.transpose(vpt[:, 0:128], vsrc[:, psl], C.ident[:]))
                else:
                    fns.append(lambda E: E.transpose(vpt[:, 128:256], vc[:, cur], C.ident[:]))
            P.group(fns, reads=[vcb, C.cb] + ([vsrcb] if has_prev else []), writes=[vptb])
            c0 = 0 if has_prev else 128
            sp3 = sp_[:].rearrange("p (h c) -> p h c", h=2)[:, :, c0:256]
            ee3 = ee[e][:].rearrange("p (h c) -> p h c", h=2)[:, :, c0:256]
            pm3 = pm[e][:].rearrange("p (h c) -> p h c", h=2)[:, :, c0:256]
            mk3 = msk[:].rearrange("p (h c) -> p h c", h=2)[:, :, pi * 256 + c0:pi * 256 + 256]
            P.op("act", lambda E: E.activation(out=ee3, in_=sp3, func=AF.Exp), reads=[spb_], writes=[eeb[e]])
            P.op("dve", lambda E: E.tensor_tensor(out=pm3, in0=ee3, in1=mk3, op=ALU.mult), reads=[eeb[e], mskb], writes=[pmb[e]])
            b0 = blks[0]
            vpt3 = vpt[:, 0:256].rearrange("p (b c) -> p b c", b=2)
            P.op("act", lambda E: E.copy(out=VA[e][:, b0:2, :], in_=vpt3[:, b0:2, :]), reads=[vptb], writes=[VAb[e]])
            return blks

        def stage2(n, blks):
            idx, jp, s, pi, d, u4, uu = units[n]
            e = n % NE
            osl = slice(uu * 128, (uu + 1) * 128)
            nb = len(blks)
            for h in range(2):
                hs = slice(h * 64, (h + 1) * 64)
                fn_n = [(lambda E, b=b, bi=bi: E.matmul(pnum[hs, osl], VA[e][:, b, hs], pm[e][:, h * 256 + b * 128:h * 256 + (b + 1) * 128], start=(bi == 0), stop=(bi == nb - 1))) for bi, b in enumerate(blks)]
                P.group(fn_n, reads=[VAb[e], pmb[e]], writes=[pnumb])
                fn_d = [(lambda E, b=b, bi=bi: E.matmul(pden[hs, osl], C.onA[:, 0:64], pm[e][:, h * 256 + b * 128:h * 256 + (b + 1) * 128], start=(bi == 0), stop=(bi == nb - 1))) for bi, b in enumerate(blks)]
                P.group(fn_d, reads=[pmb[e], C.cb], writes=[pdenb])
            if uu == 3:
                if d == 1:
                    nv = numa[:, 512 * u4:512 * (u4 + 1)]
                    dv = dena[:, 512 * u4:512 * (u4 + 1)]
                    P.op("act", lambda E: E.copy(out=nv, in_=pnum[:]), reads=[pnumb], writes=[numab])
                    P.op("dve", lambda E: E.tensor_copy(out=dv, in_=pden[:]), reads=[pdenb], writes=[denab])
                else:
                    if d == 4:
                        nv = numa[:, 512 * u4:512 * (u4 + 1)].rearrange("p (i r) -> p r i", r=4)
                        dv = dena[:, 512 * u4:512 * (u4 + 1)].rearrange("p (i r) -> p r i", r=4)
                    else:
                        nv = numa[:].rearrange("p (i r) -> p r i", r=16)[:, 4 * u4:4 * u4 + 4, :]
                        dv = dena[:].rearrange("p (i r) -> p r i", r=16)[:, 4 * u4:4 * u4 + 4, :]
                    p4 = pnum[:].rearrange("p (r i) -> p r i", r=4)
                    p5 = pden[:].rearrange("p (r i) -> p r i", r=4)
                    P.op("dve", lambda E: E.tensor_tensor(out=nv, in0=p4, in1=nv, op=ALU.add), reads=[pnumb], writes=[numab])
                    P.op("dve", lambda E: E.tensor_tensor(out=dv, in0=p5, in1=dv, op=ALU.add), reads=[pdenb], writes=[denab])

        def finalize(idx):
            jp, s = items[idx]
            yb = idx % 2
            P.op("dve", lambda E: E.reciprocal(out=dena[:], in_=dena[:]), writes=[denab])
            P.op("pool", lambda E: E.tensor_tensor(out=numa[:], in0=numa[:], in1=dena[:], op=ALU.mult), reads=[denab], writes=[numab])
            P.op("pool", lambda E: E.tensor_tensor(out=yo[yb][:], in0=numa[:], in1=sz[idx % 2][:], op=ALU.mult),
                 reads=[numab, szb[idx % 2]], writes=[yob[yb]])
            P.dma("pool", yob[yb].sem, C.yT[1024 + jp * 128:1024 + (jp + 1) * 128, s * SB:(s + 1) * SB], yo[yb][:], reads=[yob[yb]], dram_write=True)

        def item_start(idx):
            jp, s = items[idx]
            if s == 0:
                P.load(biab, bia[:].rearrange("p (h c) -> p h c", h=2), C.biasT[2 * jp:2 * jp + 2].rearrange("h p c -> p h c"))
                P.op("act", lambda E: E.activation(out=msk[:], in_=bia[:], func=AF.Exp), reads=[biab], writes=[mskb])

        loads(0)
        loads(1)
        UPI = 48
        item_start(0)
        pend = stage1(0)
        for n in range(len(units)):
            nxt = None
            if n + 1 < len(units):
                if (n + 1) % UPI == 0:
                    item_start((n + 1) // UPI)
                nxt = stage1(n + 1)
            stage2(n, pend)
            pend = nxt
            if (n + 1) % UPI == 0:
                finalize(n // UPI)
                if n // UPI + 2 < len(items):
                    loads(n // UPI + 2)
        P.barrier()
        P.emit()
        for b in knb + vvb + qnb + szb + yob + [biab]:
            P.put_sem(b.sem)


def phaseA_odd(C, l, xin):
    nc, P = C.nc, C.P
    j = l // 2
    par = C.par
    with ExitStack() as st:
        sb = lambda n, s, d: st.enter_context(nc.sbuf_tensor(U(n), s, d))
        xt = [sb("xt%d" % i, [128, 8 * TT], F32) for i in range(2)]
        xtb = [Buf("xt%d" % i, P.get_sem()) for i in range(2)]
        sq = sb("sq", [128, 8 * TT], BF16); sqb = Buf("sq")
        hn = [sb("hn%d" % i, [128, 8 * TT], BF16) for i in range(2)]
        hnb = [Buf("hn%d" % i) for i in range(2)]
        rs = sb("rs", [128, TT], F32); rsb = Buf("rs")
        wt = [sb("wt%d" % i, [128, 8 * 512], BF16) for i in range(3)]
        wtb = [Buf("wt%d" % i, P.get_sem()) for i in range(3)]
        sq_s = sb("stg_q", [128, 16 * TT], BF16); sq_b = Buf("stg_q", P.get_sem("S"))
        sz_s = sb("stg_z", [128, 16 * TT], BF16); sz_b = Buf("stg_z", P.get_sem("S"))
        sf_s = sb("stg_f", [128, 16 * TT], F32); sf_b = Buf("stg_f", P.get_sem("S"))
        si_s = sb("stg_i", [128, 4 * 2048], BF16); si_b = Buf("stg_i", P.get_sem("S"))
        NTMP = 3
        tA = [sb("tA%d" % i, [128, TT], F32) for i in range(NTMP)]; tAb = [Buf("tA") for i in range(NTMP)]
        tB = [sb("tB%d" % i, [128, TT], F32) for i in range(NTMP)]; tBb = [Buf("tB") for i in range(NTMP)]
        tC = [sb("tC%d" % i, [128, TT], F32) for i in range(NTMP)]; tCb = [Buf("tC") for i in range(NTMP)]
        banks = BankPool(C, st, 8)
        P._waits("sp", P.dram_pending)

        def xload(i):
            P.load(xtb[i % 2], xt[i % 2][:].rearrange("p (k t) -> p k t", k=8),
                   xin[:, i * TT:(i + 1) * TT].rearrange("(k p) t -> p k t", p=128))

        nw = [0]

        def wload(g):
            s = nw[0] % 3
            nw[0] += 1
            P.load(wtb[s], wt[s][:].rearrange("p (k c) -> p k c", k=8),
                   C.wbi[l][:, g * 512:(g + 1) * 512].rearrange("(k p) c -> p k c", p=128))
            return s

        xload(0)
        wq = [wload(0), wload(1)]
        nt = 0
        for i in range(NT):
            if i + 1 < NT:
                xload(i + 1)
            hb = i % 2
            emit_norm(C, l, xt[i % 2], xtb[i % 2], sq, sqb, hn[hb], hnb[hb], rs, rsb, banks)
            hn3 = hn[hb][:].rearrange("p (k t) -> p k t", k=8)
            for g in range(16):
                s = wq.pop(0)
                gi = i * 16 + g + 2
                if gi < NT * 16:
                    wq.append(wload(gi % 16))
                w3 = wt[s][:].rearrange("p (k c) -> p k c", k=8)
                if g < 12:
                    for c in range(4):
                        cc = g * 4 + c
                        h, typ = cc // 3, cc % 3
                        ps, psb = banks.get()
                        P.group([(lambda E, k=k, c=c, ps=ps: E.matmul(ps[:], w3[:, k, c * 128:(c + 1) * 128], hn3[:, k, :], start=(k == 0), stop=(k == 7))) for k in range(8)],
                                reads=[wtb[s], hnb[hb]], writes=[psb])
                        hsl = slice(h * TT, (h + 1) * TT)
                        if typ == 0:
                            P.op("act", lambda E, ps=ps, hsl=hsl: E.activation(out=sq_s[:, hsl], in_=ps[:], func=AF.Silu), reads=[psb], writes=[sq_b])
                        elif typ == 2:
                            P.op("act", lambda E, ps=ps, hsl=hsl: E.activation(out=sz_s[:, hsl], in_=ps[:], func=AF.Silu), reads=[psb], writes=[sz_b])
                        else:
                            ti = nt % NTMP
                            nt += 1
                            P.op("act", lambda E, ps=ps, ti=ti: E.activation(out=tA[ti][:], in_=ps[:], func=AF.Exp, scale=-1.0), reads=[psb], writes=[tAb[ti]])
                            P.op("act", lambda E, ti=ti, h=h: E.activation(out=tB[ti][:], in_=tA[ti][:], func=AF.Ln, bias=1.0, scale=C.lbt[:, j, h:h + 1]),
                                 reads=[tAb[ti], C.cb], writes=[tBb[ti]])
                            P.op("act", lambda E, ti=ti: E.activation(out=tC[ti][:], in_=tA[ti][:], func=AF.Ln, bias=1.0, scale=1.0), reads=[tAb[ti]], writes=[tCb[ti]])
                            P.op("dve", lambda E, ti=ti, hsl=hsl: E.tensor_tensor(out=sf_s[:, hsl], in0=tB[ti][:], in1=tC[ti][:], op=ALU.subtract),
                                 reads=[tBb[ti], tCb[ti]], writes=[sf_b])
                else:
                    cb_ = g - 12
                    for tb in range(4):
                        ps, psb = banks.get()
                        P.group([(lambda E, k=k, tb=tb, ps=ps: E.matmul(ps[:], hn3[:, k, tb * 128:(tb + 1) * 128], w3[:, k, :], start=(k == 0), stop=(k == 7))) for k in range(8)],
                                reads=[wtb[s], hnb[hb]], writes=[psb])
                        osl = slice(tb * 2048 + cb_ * 512, tb * 2048 + (cb_ + 1) * 512)
                        if tb % 2 == 0:
                            P.op("dve", lambda E, ps=ps, osl=osl: E.tensor_copy(out=si_s[:, osl], in_=ps[:]), reads=[psb], writes=[si_b])
                        else:
                            P.op("act", lambda E, ps=ps, osl=osl: E.copy(out=si_s[:, osl], in_=ps[:]), reads=[psb], writes=[si_b])
            tsl = slice(i * TT, (i + 1) * TT)
            P.dma("pool", sq_b.sem, C.t1[:, tsl].rearrange("(c p) t -> p c t", p=128), sq_s[:].rearrange("p (c t) -> p c t", c=16), reads=[sq_b], dram_write=True)
            P.dma("pool", sz_b.sem, C.t2[:, tsl].rearrange("(c p) t -> p c t", p=128), sz_s[:].rearrange("p (c t) -> p c t", c=16), reads=[sz_b], dram_write=True)
            P.dma("pool", sf_b.sem, C.t3[:, tsl].rearrange("(c p) t -> p c t", p=128), sf_s[:].rearrange("p (c t) -> p c t", c=16), reads=[sf_b], dram_write=True)
            P.dma("pool", si_b.sem, C.t4[tsl, :].rearrange("(b p) n -> p b n", p=128), si_s[:].rearrange("p (b n) -> p b n", b=4), reads=[si_b], dram_write=True)
        P.barrier()
        P.emit()
        for b in xtb + wtb + [sq_b, sz_b, sf_b, si_b]:
            P.put_sem(b.sem)


def phaseB_odd(C, l):
    nc, P = C.nc, C.P
    j = l // 2
    par = C.par
    QT = 2048
    with ExitStack() as st:
        sb = lambda n, s, d: st.enter_context(nc.sbuf_tensor(U(n), s, d))
        qs = [sb("qs%d" % i, [128, QT], BF16) for i in range(2)]; qsb = [Buf("qs", P.get_sem()) for i in range(2)]
        lf = [sb("lf%d" % i, [128, QT], F32) for i in range(2)]; lfb = [Buf("lf", P.get_sem()) for i in range(2)]
        vt = [sb("vt%d" % i, [128, 16, 128], BF16) for i in range(2)]; vtb = [Buf("vt", P.get_sem()) for i in range(2)]
        sz = [sb("sz%d" % i, [128, QT], BF16) for i in range(2)]; szb = [Buf("sz", P.get_sem()) for i in range(2)]
        yo = [sb("yo%d" % i, [128, QT], BF16) for i in range(2)]; yob = [Buf("yo", P.get_sem("S")) for i in range(2)]
        NS = 4
        gc = [sb("gc%d" % i, [128, TT], F32) for i in range(NS)]; gcb = [Buf("gc") for i in range(NS)]
        eg = [sb("eg%d" % i, [128, TT], F32) for i in range(NS)]; egb = [Buf("eg") for i in range(NS)]
        en = [sb("en%d" % i, [128, TT], F32) for i in range(NS)]; enb = [Buf("en") for i in range(NS)]
        kk = [sb("kk%d" % i, [128, TT], F32) for i in range(NS)]; kkb = [Buf("kk") for i in range(NS)]
        qt = [sb("qt%d" % i, [128, TT], BF16) for i in range(NS)]; qtb = [Buf("qt") for i in range(NS)]
        kt = [sb("kt%d" % i, [128, TT], BF16) for i in range(NS)]; ktb = [Buf("kt") for i in range(NS)]
        kh = [sb("kh%d" % i, [128, TT], BF16) for i in range(NS)]; khb = [Buf("kh") for i in range(NS)]
        km = [sb("km%d" % i, [128, 2, TT], BF16) for i in range(NS)]; kmb = [Buf("km") for i in range(NS)]
        at = [sb("at%d" % i, [128, 4, 2, 64], BF16) for i in range(NS)]; atb = [Buf("at") for i in range(NS)]
        for i in range(NS):
            P.op("pool", lambda E, i=i: E.memset(km[i][:], 0.0), writes=[kmb[i]])
            P.op("pool", lambda E, i=i: E.memset(at[i][:], 0.0), writes=[atb[i]])
        Sf = [sb("Sf%d" % i, [128, 128], F32) for i in range(2)]; Sfb = [Buf("Sf") for i in range(2)]
        Sb_ = [sb("Sb%d" % i, [128, 128], BF16) for i in range(4)]; Sbb = [Buf("Sb") for i in range(4)]
        osb = [sb("osb%d" % i, [128, TT], F32) for i in range(2)]; osbb = [Buf("osb") for i in range(2)]
        osq = sb("osq", [128, TT], BF16); osqb = Buf("osq")
        rr = sb("rr", [128, TT], F32); rrb = Buf("rr")
        SC = [st.enter_context(nc.psum_tensor(U("SC"), [128, 512], F32)) for i in range(1)]; SCb = [Buf("SC") for i in range(1)]
        TP = st.enter_context(nc.psum_tensor(U("TP"), [128, 1024], BF16)); TPb = Buf("TP")
        DS = [st.enter_context(nc.psum_tensor(U("DS"), [128, 512], F32)) for i in range(4)]; DSb = [Buf("DS") for i in range(4)]
        OP = st.enter_context(nc.psum_tensor(U("OP"), [128, 512], F32)); OPb = Buf("OP")
        MS = st.enter_context(nc.psum_tensor(U("MS"), [128, 512], F32)); MSb = Buf("MS")
        P._waits("sp", P.dram_pending)
        items = [(h, qd) for h in range(16) for qd in range(4)]
        NSEG = len(items) * 4

        def loads(idx):
            h, qd = items[idx]
            r = slice(h * 128, (h + 1) * 128)
            tsl = slice(qd * QT, (qd + 1) * QT)
            P.load(qsb[idx % 2], qs[idx % 2][:], C.t1[r, tsl])
            P.load(lfb[idx % 2], lf[idx % 2][:], C.t3[r, tsl])
            P.load(vtb[idx % 2], vt[idx % 2][:], C.t4[tsl, r].rearrange("(b p) n -> p b n", p=128))
            P.load(szb[idx % 2], sz[idx % 2][:], C.t2[r, tsl])

        def prep(n):
            idx, sg = n // 4, n % 4
            ib, ss = idx % 2, n % NS
            seg = slice(sg * TT, (sg + 1) * TT)
            for c in range(8):
                cs = slice(c * 64, (c + 1) * 64)
                P.op("dve", lambda E, cs=cs, c=c: E.tensor_tensor_scan(out=gc[ss][:, cs], data0=C.ones_f[:], data1=lf[ib][:, sg * TT + c * 64:sg * TT + (c + 1) * 64],
                                                                     initial=0.0, op0=ALU.mult, op1=ALU.add),
                     reads=[lfb[ib], C.cb], writes=[gcb[ss]])
            P.op("act", lambda E: E.activation(out=eg[ss][:], in_=gc[ss][:], func=AF.Exp), reads=[gcb[ss]], writes=[egb[ss]])
            P.op("act", lambda E: E.activation(out=en[ss][:], in_=gc[ss][:], func=AF.Exp, scale=-1.0), reads=[gcb[ss]], writes=[enb[ss]])
            P.op("act", lambda E: E.activation(out=kk[ss][:], in_=lf[ib][:, seg], func=AF.Exp), reads=[lfb[ib]], writes=[kkb[ss]])
            P.op("pool", lambda E: E.tensor_scalar(out=kk[ss][:], in0=kk[ss][:], scalar1=-1.0, scalar2=1.0, op0=ALU.mult, op1=ALU.add), writes=[kkb[ss]])
            P.op("dve", lambda E: E.tensor_tensor(out=qt[ss][:], in0=qs[ib][:, seg], in1=eg[ss][:], op=ALU.mult), reads=[qsb[ib], egb[ss]], writes=[qtb[ss]])
            P.op("pool", lambda E: E.tensor_tensor(out=kt[ss][:], in0=kk[ss][:], in1=en[ss][:], op=ALU.mult), reads=[kkb[ss], enb[ss]], writes=[ktb[ss]])
            kt3 = kt[ss][:].rearrange("p (c t) -> p c t", c=8)
            kh3 = kh[ss][:].rearrange("p (c t) -> p c t", c=8)
            egl = eg[ss][:, 63:TT:64].unsqueeze(2).broadcast_to([128, 8, 64])
            P.op("pool", lambda E: E.tensor_tensor(out=kh3, in0=kt3, in1=egl, op=ALU.mult), reads=[ktb[ss], egb[ss]], writes=[khb[ss]])

        def s12(n):
            idx, sg = n // 4, n % 4
            ib, ss = idx % 2, n % NS
            P.group([(lambda E, p=p: E.transpose(TP[:, p * 128:(p + 1) * 128], kh[ss][:, p * 128:(p + 1) * 128], C.ident[:])) for p in range(4)],
                    reads=[khb[ss], C.cb], writes=[TPb])
            P.op("act", lambda E: E.copy(out=km[ss][0:64, 0, :], in_=TP[0:64, 0:512]), reads=[TPb], writes=[kmb[ss]])
            P.op("act", lambda E: E.copy(out=km[ss][64:128, 1, :], in_=TP[64:128, 0:512]), reads=[TPb], writes=[kmb[ss]])
            sc, scb = SC[0], SCb[0]
            P.group([(lambda E, c=c: E.matmul(sc[(c % 2) * 64:(c % 2) * 64 + 64, (c // 2) * 64:(c // 2) * 64 + 64], kt[ss][:, c * 64:(c + 1) * 64], qt[ss][:, c * 64:(c + 1) * 64], start=True, stop=True)) for c in range(8)],
                    reads=[ktb[ss], qtb[ss]], writes=[scb])
            sc3 = sc[:, 0:256].rearrange("p (a t) -> p a t", a=4)
            tri3 = C.tri[:].rearrange("p (a t) -> p a t", a=4)
            P.op("dve", lambda E: E.tensor_tensor(out=at[ss][0:64, :, 0, :], in0=sc3[0:64], in1=tri3[0:64], op=ALU.mult), reads=[scb, C.cb], writes=[atb[ss]])
            P.op("dve", lambda E: E.tensor_tensor(out=at[ss][64:128, :, 1, :], in0=sc3[64:128], in1=tri3[64:128], op=ALU.mult), reads=[scb, C.cb], writes=[atb[ss]])

        def s2(n):
            idx, sg = n // 4, n % 4
            ib, ss = idx % 2, n % NS
            for half in range(2):
                k = (2 * n + half) % 4
                fns = []
                for c in range(half * 4, half * 4 + 4):
                    prs = slice((c % 2) * 64, (c % 2) * 64 + 64)
                    blk = (sg * TT + c * 64) // 128
                    pr = c // 2
                    fns.append(lambda E, c=c, prs=prs, blk=blk, pr=pr: E.matmul(DS[k][:, (c % 4) * 128:(c % 4) * 128 + 128], km[ss][:, c % 2, pr * 128:(pr + 1) * 128], vt[ib][:, blk, :], start=True, stop=True))
                P.group(fns, reads=[kmb[ss], vtb[ib]], writes=[DSb[k]])

        nch = [0]

        def s3(n):
            idx, sg = n // 4, n % 4
            h, qd = items[idx]
            ib, ss = idx % 2, n % NS
            if qd == 0 and sg == 0:
                P.op("dve", lambda E: E.memset(Sf[nch[0] % 2][:], 0.0), writes=[Sfb[nch[0] % 2]])
                sbi0 = nch[0] % 4
                P.op("pool", lambda E: E.memset(Sb_[sbi0][:], 0.0), writes=[Sbb[sbi0]])
            for c in range(8):
                prs = slice((c % 2) * 64, (c % 2) * 64 + 64)
                cs = slice(c * 64, (c + 1) * 64)
                blk = (sg * TT + c * 64) // 128
                k = (2 * n + c // 4) % 4
                sbi = nch[0] % 4
                sbn = (nch[0] + 1) % 4
                nch[0] += 1
                P.group([lambda E, prs=prs, blk=blk, cs=cs, c=c: E.matmul(OP[:, cs], vt[ib][:, blk, :], at[ss][:, c // 2, c % 2, :], start=True, stop=False),
                         lambda E, sbi=sbi, cs=cs: E.matmul(OP[:, cs], Sb_[sbi][:], qt[ss][:, cs], start=False, stop=True)],
                        reads=[vtb[ib], atb[ss], Sbb[sbi], qtb[ss]], writes=[OPb])
                P.op("dve", lambda E, c=c, k=k, sbi=sbi, sbn=sbn: E.scalar_tensor_tensor(out=Sf[sbn % 2][:], in0=Sf[sbi % 2][:], scalar=eg[ss][:, c * 64 + 63:c * 64 + 64], in1=DS[k][:, (c % 4) * 128:(c % 4) * 128 + 128], op0=ALU.mult, op1=ALU.add),
                     reads=[DSb[k], egb[ss], Sfb[sbi % 2]], writes=[Sfb[sbn % 2]])
                P.op("act", lambda E, sbn=sbn: E.copy(out=Sb_[sbn][:], in_=Sf[sbn % 2][:]), reads=[Sfb[sbn % 2]], writes=[Sbb[sbn]])
            ob = n % 2
            P.op("act", lambda E: E.activation(out=osq[:], in_=OP[:], func=AF.Square), reads=[OPb], writes=[osqb])
            P.op("act", lambda E: E.activation(out=osb[ob][:], in_=OP[:], func=AF.Identity, scale=par[:, PC_OG + j * 16 + h:PC_OG + j * 16 + h + 1]), reads=[OPb, C.cb], writes=[osbb[ob]])

        def epiB(n):
            idx, sg = n // 4, n % 4
            h, qd = items[idx]
            ib = idx % 2
            ob = n % 2
            seg = slice(sg * TT, (sg + 1) * TT)
            P.group([lambda E: E.matmul(MS[:], C.ones_h[:], osq[:], start=True, stop=True)], reads=[osqb, C.cb], writes=[MSb])
            P.op("act", lambda E: E.activation(out=rr[:], in_=MS[:], func=AF.Ln, bias=EPS, scale=1.0), reads=[MSb], writes=[rrb])
            P.op("act", lambda E: E.activation(out=rr[:], in_=rr[:], func=AF.Exp, scale=-0.5), writes=[rrb])
            P.op("pool", lambda E: E.tensor_tensor(out=rr[:], in0=rr[:], in1=sz[ib][:, seg], op=ALU.mult), reads=[szb[ib]], writes=[rrb])
            P.op("pool", lambda E: E.tensor_tensor(out=yo[ib][:, seg], in0=osb[ob][:], in1=rr[:], op=ALU.mult), reads=[osbb[ob], rrb], writes=[yob[ib]])
            if sg == 3:
                P.dma("pool", yob[ib].sem, C.yT[h * 128:(h + 1) * 128, qd * QT:(qd + 1) * QT], yo[ib][:], reads=[yob[ib]], dram_write=True)
                if idx + 2 < len(items):
                    loads(idx + 2)

        loads(0)
        loads(1)
        prep(0)
        prep(1)
        s12(0)
        s2(0)
        for n in range(NSEG):
            if n + 2 < NSEG:
                prep(n + 2)
            if n + 1 < NSEG:
                s12(n + 1)
            if n > 0:
                epiB(n - 1)
            s3(n)
            if n + 1 < NSEG:
                s2(n + 1)
        epiB(NSEG - 1)
        P.barrier()
        P.emit()
        for b in qsb + lfb + vtb + szb + yob:
            P.put_sem(b.sem)


def phaseC(C, l, xin, xout):
    nc, P = C.nc, C.P
    with ExitStack() as st:
        sb = lambda n, s, d: st.enter_context(nc.sbuf_tensor(U(n), s, d))
        wo = sb("wo", [128, 16 * 1024], BF16); wob = Buf("wo", P.get_sem())
        yt = [sb("yt%d" % i, [128, 16 * TT], BF16) for i in range(2)]; ytb = [Buf("yt", P.get_sem()) for i in range(2)]
        xt = [sb("xt%d" % i, [128, 8 * TT], F32) for i in range(2)]; xtb = [Buf("xt", P.get_sem()) for i in range(2)]
        xo = [sb("xo%d" % i, [128, 8 * TT], F32) for i in range(2)]; xob = [Buf("xo", P.get_sem("S")) for i in range(2)]
        banks = BankPool(C, st, 8)
        P._waits("sp", P.dram_pending)
        P.load(wob, wo[:].rearrange("p (k c) -> p k c", k=16), C.wbo[l].rearrange("(k p) c -> p k c", p=128))
        wo3 = wo[:].rearrange("p (k c) -> p k c", k=16)

        def loads(i):
            tsl = slice(i * TT, (i + 1) * TT)
            P.load(ytb[i % 2], yt[i % 2][:].rearrange("p (k t) -> p k t", k=16), C.yT[:, tsl].rearrange("(k p) t -> p k t", p=128))
            P.load(xtb[i % 2], xt[i % 2][:].rearrange("p (k t) -> p k t", k=8), xin[:, tsl].rearrange("(k p) t -> p k t", p=128))

        loads(0)
        for i in range(NT):
            if i + 1 < NT:
                loads(i + 1)
            b = i % 2
            y3 = yt[b][:].rearrange("p (k t) -> p k t", k=16)
            for oc in range(8):
                ps, psb = banks.get()
                P.group([(lambda E, k=k, oc=oc, ps=ps: E.matmul(ps[:], wo3[:, k, oc * 128:(oc + 1) * 128], y3[:, k, :], start=(k == 0), stop=(k == 15))) for k in range(16)],
                        reads=[wob, ytb[b]], writes=[psb])
                osl = slice(oc * TT, (oc + 1) * TT)
                P.op("dve", lambda E, ps=ps, osl=osl, b=b: E.tensor_tensor(out=xo[b][:, osl], in0=ps[:], in1=xt[b][:, osl], op=ALU.add),
                     reads=[psb, xtb[b]], writes=[xob[b]])
            P.dma("pool", xob[b].sem, xout[:, i * TT:(i + 1) * TT].rearrange("(k p) t -> p k t", p=128), xo[b][:].rearrange("p (k t) -> p k t", k=8),
                  reads=[xob[b]], dram_write=True)
        P.barrier()
        P.emit()
        for b_ in [wob] + ytb + xtb + xob:
            P.put_sem(b_.sem)


def _t5_bucket(distance):
    max_exact = 16
    dist = np.maximum(distance, max_exact).astype(np.float32)
    scaled = np.log(dist / np.float32(max_exact)) / np.float32(math.log(2048 / max_exact))
    large = np.minimum(max_exact + (scaled.astype(np.float32) * np.float32(16)).astype(np.int32), 31)
    return np.where(distance < max_exact, distance, large)


def _host_tables(inputs):
    f32 = np.float32
    par = np.zeros((128, NPAR), f32)
    for l in range(NL):
        ln = inputs["ln_even"][l // 2] if l % 2 == 0 else inputs["ln_odd"][l // 2]
        par[:, PC_LN + l * 8:PC_LN + (l + 1) * 8] = np.asarray(ln, f32).reshape(8, 128).T
    cw = np.asarray(inputs["conv_w"], f32)
    for j in range(2):
        for k in range(3):
            par[:, PC_CONV + (j * 3 + k) * 8:PC_CONV + (j * 3 + k + 1) * 8] = cw[j, k].reshape(8, 128).T
    for j in range(2):
        par[:, PC_QG + j] = np.tile(np.asarray(inputs["q_gain"], f32)[j], 2)
        par[:, PC_KG + j] = np.tile(np.asarray(inputs["k_gain"], f32)[j], 2)
        par[:, PC_LB + j * 16:PC_LB + (j + 1) * 16] = np.asarray(inputs["lower_bounds"], f32)[j].reshape(16, 128).T
        par[:, PC_OG + j * 16:PC_OG + (j + 1) * 16] = np.asarray(inputs["o_gain"], f32)[j].reshape(16, 128).T
    rb = np.asarray(inputs["rel_bias"], f32)
    kj = np.arange(128)[:, None]
    qi = np.arange(128)[None, :]
    bias = np.full((16, 128, 768), NEG, f32)
    for pi, d in enumerate((1, 4, 16)):
        for blk in range(2):
            delta = (128 if blk == 0 else 0) + qi - kj
            valid = (delta >= 0) & (delta <= 128)
            bucket = _t5_bucket(np.maximum(delta, 0) * d)
            tab = rb[bucket]
            tab = np.where(valid[:, :, None], tab, f32(NEG))
            bias[:, :, pi * 256 + blk * 128:pi * 256 + (blk + 1) * 128] = tab.transpose(2, 0, 1)
    s_ = np.arange(128)[:, None] % 64
    t_ = np.arange(256)[None, :] % 64
    tri = np.ascontiguousarray((s_ <= t_).astype(f32))
    return par, bias, tri


_NC_CACHE = {}


def kernel(x, ln_even, w_in_even, conv_w, q_gain, k_gain, w_out_even, rel_bias,
           ln_odd, w_in_odd, lower_bounds, o_gain, w_out_odd, _nlayers=NL):
    inputs = dict(x=x, ln_even=ln_even, w_in_even=w_in_even, conv_w=conv_w, q_gain=q_gain, k_gain=k_gain,
                  w_out_even=w_out_even, rel_bias=rel_bias, ln_odd=ln_odd, w_in_odd=w_in_odd,
                  lower_bounds=lower_bounds, o_gain=o_gain, w_out_odd=w_out_odd)
    par, bias, tri = _host_tables(inputs)
    x = np.asarray(x, np.float32)
    B = x.shape[0]
    if _nlayers not in _NC_CACHE:
        _NC_CACHE[_nlayers] = build(_nlayers)
    nc = _NC_CACHE[_nlayers]
    common = dict(w_in_even=np.ascontiguousarray(w_in_even, np.float32), w_in_odd=np.ascontiguousarray(w_in_odd, np.float32),
                  w_out_even=np.ascontiguousarray(w_out_even, np.float32), w_out_odd=np.ascontiguousarray(w_out_odd, np.float32),
                  par=par, biasT=bias, tri=tri)
    in_maps = []
    for b in range(B):
        m = dict(common)
        m["xT"] = np.ascontiguousarray(x[b].T)
        in_maps.append(m)
    res = run_bass_kernel_spmd(nc, in_maps, core_ids=list(range(B)))
    out = np.stack([np.ascontiguousarray(r["outT"].T) for r in res.results], axis=0)
    return out.astype(np.float32)
```
